# Optimizing a Trainium2 kernel written in Bass

```python
import math
import jax, jax.numpy as jnp
from jax import lax
import numpy as np

D_MODEL = 1024
BATCH = 8
SEQ = 2048
DEPTH = 1
DEC_BATCH = 32
DEC_SEQ = 8
PAST_LEN = 8192
PAGE_SIZE = 128

D_MIX = D_MODEL
D_CONV = D_MIX // 2
CONV_GROUPS = 8
CONV_WIDTH = 3
N_HEADS = 4
D_HEAD = 64
D_V = 2 * D_HEAD
D_ATTN = N_HEADS * D_V
D_QK = N_HEADS * 2 * D_HEAD
D_IN = 3 * D_CONV + 2 * D_QK + D_ATTN
D_FF = 4 * D_MODEL
Q_BLOCK = 128
EPS = 1e-5
NEG = -1e30

kernel_name = 'hymba_conv_diffattn_alibi_step'


def rmsnorm(x, g):
    xf = x.astype(jnp.float32)
    y = xf * lax.rsqrt(jnp.mean(xf * xf, axis=-1, keepdims=True) + EPS)
    return (y * g.astype(jnp.float32)).astype(x.dtype)


def alibi_slopes():
    return jnp.asarray([2.0 ** (-8.0 * (h + 1) / N_HEADS) for h in range(N_HEADS)], jnp.float32)


def project(h, w_in):
    z = h @ w_in
    cuts = [D_CONV, 2 * D_CONV, 3 * D_CONV, 3 * D_CONV + D_QK, 3 * D_CONV + 2 * D_QK]
    b, c, u, q, k, v = jnp.split(z, cuts, axis=-1)
    bsz, t = h.shape[:2]
    q = q.reshape(bsz, t, N_HEADS, 2, D_HEAD)
    k = k.reshape(bsz, t, N_HEADS, 2, D_HEAD)
    v = v.reshape(bsz, t, N_HEADS, D_V)
    return b, c, u, q, k, v


def short_conv(b, c, u, state, w):
    cu = c * u
    ext = jnp.concatenate([state.astype(cu.dtype), cu], axis=1)
    t = cu.shape[1]
    y = sum(ext[:, j:j + t] * w[j] for j in range(CONV_WIDTH))
    return b * y, ext[:, ext.shape[1] - (CONV_WIDTH - 1):]


def diff_attn(q, k, v, q_pos, k_pos, lam):
    s = jnp.einsum('bqhcd,bkhcd->bchqk', q, k, preferred_element_type=jnp.float32) * (D_HEAD ** -0.5)
    dist = q_pos[:, None] - k_pos[None, :]
    bias = -alibi_slopes()[:, None, None] * dist.astype(jnp.float32)[None]
    s = jnp.where((dist >= 0)[None, None, None], s + bias[None, None], NEG)
    p = jax.nn.softmax(s, axis=-1)
    a = p[:, 0] - lam * p[:, 1]
    return jnp.einsum('bhqk,bkhd->bqhd', a.astype(v.dtype), v)


def prompt_attention(q, k, v, lam):
    bsz, t = q.shape[:2]
    n_blocks = t // Q_BLOCK
    k_pos = jnp.arange(t)

    def block(i):
        qs = lax.dynamic_slice_in_dim(q, i * Q_BLOCK, Q_BLOCK, axis=1)
        return diff_attn(qs, k, v, i * Q_BLOCK + jnp.arange(Q_BLOCK), k_pos, lam)

    o = lax.map(block, jnp.arange(n_blocks))
    return jnp.moveaxis(o, 0, 1).reshape(bsz, t, N_HEADS, D_V)


def sample_attention(q, k, v, lam, k_pool, v_pool, page_table):
    bd, tn = q.shape[:2]
    past = page_table.shape[1] * PAGE_SIZE
    kp = k_pool[page_table].reshape(bd, past, N_HEADS, 2, D_HEAD)
    vp = v_pool[page_table].reshape(bd, past, N_HEADS, D_V)
    k_all = jnp.concatenate([kp.astype(k.dtype), k], axis=1)
    v_all = jnp.concatenate([vp.astype(v.dtype), v], axis=1)
    return diff_attn(q, k_all, v_all, past + jnp.arange(tn), jnp.arange(past + tn), lam)


def block_forward(x, conv_state, attend, norm1_g, w_in, conv_w, lam, lam_init, subln_g,
                  w_out, norm2_g, w_up, w_down):
    h = rmsnorm(x, norm1_g)
    b, c, u, q, k, v = project(h, w_in)
    y_conv, new_conv = short_conv(b, c, u, conv_state, conv_w)
    o = attend(q, k, v, lam)
    o = rmsnorm(o, subln_g) * (1.0 - lam_init)
    o = o.reshape(o.shape[0], o.shape[1], D_ATTN)
    x = x + jnp.concatenate([y_conv, o.astype(y_conv.dtype)], axis=-1) @ w_out
    h2 = rmsnorm(x, norm2_g)
    x = x + jnp.square(jax.nn.relu(h2 @ w_up)) @ w_down
    return x, k, v, new_conv


def setup_inputs(seed: int = 0) -> dict:
    key = jax.random.key(seed)
    ks = jax.random.split(key, 24)
    n_pages = PAST_LEN // PAGE_SIZE
    n_used = DEC_BATCH * n_pages
    n_pool = (n_used * 5) // 4
    f32 = jnp.float32
    nrm = lambda k, s, sc: jax.random.normal(k, s, f32) * sc
    page_table = jax.random.permutation(ks[5], n_pool)[:n_used].reshape(DEC_BATCH, n_pages).astype(jnp.int32)
    return {
        'x_prompt': nrm(ks[0], (BATCH, SEQ, D_MODEL), 1.0),
        'x_sample': nrm(ks[1], (DEC_BATCH, DEC_SEQ, D_MODEL), 1.0),
        'cache_k': nrm(ks[2], (DEPTH, n_pool, PAGE_SIZE, N_HEADS, 2, D_HEAD), 1.0),
        'cache_v': nrm(ks[3], (DEPTH, n_pool, PAGE_SIZE, N_HEADS, D_V), 1.0),
        'state_conv': nrm(ks[4], (DEPTH, DEC_BATCH, CONV_WIDTH - 1, D_CONV), 1.0),
        'page_table': page_table,
        'norm1_g': 1.0 + nrm(ks[6], (DEPTH, D_MODEL), 0.02),
        'w_in': nrm(ks[7], (DEPTH, D_MODEL, D_IN), D_MODEL ** -0.5),
        'conv_w': nrm(ks[8], (DEPTH, CONV_WIDTH, D_CONV), CONV_WIDTH ** -0.5),
        'lambda_q1': nrm(ks[9], (DEPTH, D_HEAD), 0.1),
        'lambda_k1': nrm(ks[10], (DEPTH, D_HEAD), 0.1),
        'lambda_q2': nrm(ks[11], (DEPTH, D_HEAD), 0.1),
        'lambda_k2': nrm(ks[12], (DEPTH, D_HEAD), 0.1),
        'subln_g': 1.0 + nrm(ks[13], (DEPTH, D_V), 0.02),
        'w_out': nrm(ks[14], (DEPTH, D_MIX, D_MODEL), D_MIX ** -0.5),
        'norm2_g': 1.0 + nrm(ks[15], (DEPTH, D_MODEL), 0.02),
        'w_up': nrm(ks[16], (DEPTH, D_MODEL, D_FF), D_MODEL ** -0.5),
        'w_down': nrm(ks[17], (DEPTH, D_FF, D_MODEL), D_FF ** -0.5),
        'final_g': 1.0 + nrm(ks[18], (D_MODEL,), 0.02),
    }


def reference(x_prompt, x_sample, cache_k, cache_v, state_conv, page_table,
              norm1_g, w_in, conv_w, lambda_q1, lambda_k1, lambda_q2, lambda_k2,
              subln_g, w_out, norm2_g, w_up, w_down, final_g):
    xp, xs = x_prompt, x_sample
    kp_l, vp_l, cp_l, ks_l, vs_l, cs_l = [], [], [], [], [], []
    for l in range(DEPTH):
        lam_init = 0.8 - 0.6 * math.exp(-0.3 * l)
        lam = (jnp.exp(jnp.sum(lambda_q1[l].astype(jnp.float32) * lambda_k1[l].astype(jnp.float32)))
               - jnp.exp(jnp.sum(lambda_q2[l].astype(jnp.float32) * lambda_k2[l].astype(jnp.float32)))
               + lam_init)
        params = (norm1_g[l], w_in[l], conv_w[l], lam, lam_init, subln_g[l],
                  w_out[l], norm2_g[l], w_up[l], w_down[l])
        zero_state = jnp.zeros((xp.shape[0], CONV_WIDTH - 1, D_CONV), xp.dtype)
        xp, kp, vp, cp = block_forward(xp, zero_state, prompt_attention, *params)
        k_pool, v_pool = cache_k[l], cache_v[l]
        attend_s = lambda q, k, v, lm: sample_attention(q, k, v, lm, k_pool, v_pool, page_table)
        xs, ks_, vs_, cs_ = block_forward(xs, state_conv[l], attend_s, *params)
        kp_l.append(kp); vp_l.append(vp); cp_l.append(cp)
        ks_l.append(ks_); vs_l.append(vs_); cs_l.append(cs_)
    y_prompt = rmsnorm(xp, final_g)
    y_sample = rmsnorm(xs, final_g)
    return (y_prompt, y_sample, jnp.stack(kp_l), jnp.stack(vp_l), jnp.stack(cp_l),
            jnp.stack(ks_l), jnp.stack(vs_l), jnp.stack(cs_l))
```

```python
import numpy as np
from contextlib import ExitStack
import concourse.bass as bass
import concourse.mybir as mybir
from concourse.bass_utils import run_bass_kernel_spmd

F32 = mybir.dt.float32
BF16 = mybir.dt.bfloat16
I32 = mybir.dt.int32
AF = mybir.ActivationFunctionType
ALU = mybir.AluOpType

ENGS = ("pe", "act", "dve", "pool", "sp")
NCORES = 8
D = 1024
S = 2048
NT = 16
DIN = 3072
DFF = 4096
NPAGE = 64
PAST = 8192
NPOOL = 2560
EPS = 1e-5
SLOPES = [2.0 ** (-8.0 * (h + 1) / 4) for h in range(4)]
LAM_INIT = 0.2
NEGBIG = -30000.0
import os as _os
CUT = int(_os.environ.get('KCUT', '100000000'))


class Op:
    __slots__ = ("eng", "fn", "idx", "waits", "flag", "dma_key", "dma_val", "vc")

    def __init__(self, eng, fn, idx):
        self.eng = eng
        self.fn = fn
        self.idx = idx
        self.waits = []
        self.flag = False
        self.dma_key = None
        self.dma_val = 0
        self.vc = None


class Prog:
    def __init__(self):
        self.ops = {e: [] for e in ENGS}
        self.vc = {e: {} for e in ENGS}
        self.last_w = {}
        self.readers = {}
        self.dma_cnt = {}

    def _dep(self, op, p):
        e = op.eng
        vc = self.vc[e]
        if p.dma_key is not None:
            k = ("dma", p.dma_key)
            if vc.get(k, 0) >= p.dma_val:
                return
            tot = self.dma_cnt[p.dma_key]
            op.waits.append(("dma", p.dma_key, tot))
            vc[k] = tot
        else:
            if p.eng == e and e == "pe":
                return
            if vc.get(p.eng, -1) >= p.idx:
                return
            op.waits.append(("eng", p.eng, p))
            p.flag = True
        for k, v in p.vc.items():
            if vc.get(k, -1) < v:
                vc[k] = v

    def add(self, eng, fn, reads=(), writes=(), dma_key=None):
        self.nadd = getattr(self, "nadd", 0) + 1
        if self.nadd > CUT:
            return None
        ops = self.ops[eng]
        op = Op(eng, fn, len(ops))
        deps = []
        for b in reads:
            w = self.last_w.get(b)
            if w is not None:
                deps.append(w)
        for b in writes:
            w = self.last_w.get(b)
            if w is not None:
                deps.append(w)
            deps.extend(self.readers.get(b, ()))
        seen = set()
        for p in deps:
            if id(p) in seen:
                continue
            seen.add(id(p))
            self._dep(op, p)
        if dma_key is not None:
            op.dma_key = dma_key
            self.dma_cnt[dma_key] = self.dma_cnt.get(dma_key, 0) + 16
            op.dma_val = self.dma_cnt[dma_key]
        vc = self.vc[eng]
        if eng == "pe":
            vc[eng] = op.idx
        snap = dict(vc)
        snap[eng] = op.idx
        if dma_key is not None:
            snap[("dma", dma_key)] = op.dma_val
        op.vc = snap
        for b in reads:
            self.readers.setdefault(b, []).append(op)
        for b in writes:
            self.last_w[b] = op
            self.readers[b] = []
        ops.append(op)
        return op

    def emit(self, nc, es, final_eng="sp"):
        sem_e = {e: es.enter_context(nc.semaphore("c_" + e)) for e in ENGS}
        sem_d = {}
        for i, k in enumerate(self.dma_cnt):
            sem_d[k] = es.enter_context(nc.semaphore("d%d" % i))
        rank = {}
        for e in ENGS:
            r = 0
            for op in self.ops[e]:
                if op.flag:
                    r += 1
                    rank[id(op)] = r
        finals = list(self.dma_cnt.items())

        def run(e, h):
            for op in self.ops[e]:
                for w in op.waits:
                    if w[0] == "dma":
                        h.wait_ge(sem_d[w[1]], w[2])
                    else:
                        h.wait_ge(sem_e[w[1]], rank[id(w[2])])
                ins = op.fn(h)
                if op.dma_key is not None:
                    ins.then_inc(sem_d[op.dma_key], 16)
                elif op.flag:
                    ins.then_inc(sem_e[e], 1)
            if e == final_eng:
                for k, v in finals:
                    h.wait_ge(sem_d[k], v)

        with nc.Block() as block:
            @block.tensor
            def _(h):
                run("pe", h)

            @block.scalar
            def _(h):
                run("act", h)

            @block.vector
            def _(h):
                run("dve", h)

            @block.gpsimd
            def _(h):
                run("pool", h)

            @block.sync
            def _(h):
                run("sp", h)


C_TABP = 0
C_MASK = 64
C_IDENT = 192
C_SEL = 320
C_M1 = 328
C_M2 = 329
C_NEWB = 330
C_AUXR = 338
C_AUXL = 850
C_TABP2 = 850 + 1024
C_N = 850 + 1024 + 76


def make_ctab():
    t = np.zeros((128, C_N), np.float32)
    p = np.arange(128, dtype=np.float32)
    for h in range(4):
        for d in range(16):
            t[:, C_TABP + h * 16 + d] = SLOPES[h] * (p - 128.0 * d)
    for h in range(4):
        for d in range(-3, 16):
            t[:, C_TABP2 + h * 19 + d + 3] = SLOPES[h] * (p - 128.0 * d)
    k = np.arange(128)[:, None]
    q = np.arange(128)[None, :]
    t[:, C_MASK:C_MASK + 128] = (q >= k).astype(np.float32)
    t[:, C_IDENT:C_IDENT + 128] = np.eye(128, dtype=np.float32)
    t[0:8, C_SEL:C_SEL + 8] = np.eye(8)
    t[8:16, C_SEL:C_SEL + 8] = np.eye(8)
    t[0:8, C_M1] = 1.0
    t[8:16, C_M2] = 1.0
    for h in range(4):
        for c in range(2):
            for qq in range(8):
                r = h * 16 + c * 8 + qq
                for kk in range(8):
                    t[r, C_NEWB + kk] = SLOPES[h] * kk + (NEGBIG if kk > qq else 0.0)
    col = np.arange(512)
    t[0, C_AUXR:C_AUXR + 512] = 1.0
    t[1, C_AUXR:C_AUXR + 512] = 128.0 * (col // 128)
    t[2, C_AUXR:C_AUXR + 512] = (col % 128)
    for g in range(16):
        for h in range(4):
            base = C_AUXL + g * 64 + h * 16
            t[0, base:base + 16] = SLOPES[h] * 512.0 * (g - 16)
            t[1, base:base + 16] = SLOPES[h]
            t[2, base:base + 16] = SLOPES[h]
    return t


def build(stage=9, kv_rows=NPOOL * 128, skip=()):
    nc = bass.Bass("TRN2", target_bir_lowering=False)

    def din(name, shape, dt=F32):
        return nc.dram_tensor(name, list(shape), dt, kind="ExternalInput").ap()

    def dout(name, shape):
        return nc.dram_tensor(name, list(shape), F32, kind="ExternalOutput").ap()

    xp = din("xp", [S, D])
    xs = din("xs", [32, D])
    ckv = din("ckv", [kv_rows, 1024])
    sconv = din("sconv", [4, 2, 512])
    ptab = din("ptab", [1, 256], I32)
    n1g = din("n1g", [1, D])
    w_in = din("w_in", [D, DIN])
    conv_w = din("conv_w", [3, 512])
    lamv = din("lamv", [1, 256])
    subg = din("subg", [1, 128])
    w_out = din("w_out", [D, D])
    n2g = din("n2g", [1, D])
    w_up = din("w_up", [D, DFF])
    w_down = din("w_down", [DFF, D])
    fing = din("fing", [1, D])
    ctab = din("ctab", [128, C_N])

    yp = dout("yp", [S, D])
    ys = dout("ys", [32, D])
    kp = dout("kp", [S, 512])
    vp = dout("vp", [S, 512])
    cp = dout("cp", [2, 512])
    ks = dout("ks", [32, 512])
    vs = dout("vs", [32, 512])
    cs = dout("cs", [4, 2, 512])

    es = ExitStack()
    with es:
        ARENA_B = 206 * 1024
        arena = es.enter_context(nc.sbuf_tensor("arena", [128, ARENA_B // 2], BF16))
        psb = [es.enter_context(nc.psum_tensor("ps%d" % i, [128, 512], F32)) for i in range(8)]

        def R(off, shape, dt):
            n = 1
            for s_ in shape[1:]:
                n *= s_
            esz = 4 if dt in (F32, I32) else 2
            assert off % 4 == 0
            assert off + n * esz <= ARENA_B, (off, shape)
            v = arena[:, off // 2: off // 2 + n * esz // 2]
            if esz == 4:
                v = v.bitcast(dt)
            if len(shape) == 3:
                v = v.rearrange("p (a b) -> p a b", a=shape[1])
            elif len(shape) == 4:
                v = v.rearrange("p (a b c) -> p a b c", a=shape[1], b=shape[2])
            return v

        class Alloc:
            def __init__(self, base):
                self.o = base

            def __call__(self, shape, dt):
                n = 1
                for s_ in shape[1:]:
                    n *= s_
                nb = n * (4 if dt in (F32, I32) else 2)
                nb = (nb + 31) // 32 * 32
                v = R(self.o, shape, dt)
                self.o += nb
                return v

        KB = 1024
        A = Alloc(0)
        ctf = A([128, C_N], F32)
        identb = A([128, 128], BF16)
        maskb = A([128, 128], BF16)
        auxb = A([128, 512 + 1024], BF16)
        newbb = A([128, 8], BF16)
        gbc = A([128, 1024], F32)
        sgbc = A([128, 128], F32)
        convw = A([128, 4, 3], F32)
        lamb = A([128, 256], F32)
        small = A([128, 64], F32)
        ptb = A([128, 256], I32)
        idx = A([128, 256], I32)
        iot = A([128, 2], F32)
        x1s = A([128, 1024], F32)
        hTs = A([128, 8, 32], BF16)
        mixTs = A([128, 8, 32], BF16)
        h2Ts = A([128, 8, 32], BF16)
        qbd = A([128, 4, 4, 64], BF16)
        kTn = A([128, 4, 32], BF16)
        vaugn = A([128, 4, 4, 130], BF16)
        cus = A([128, 4, 4, 10], F32)
        PERSIST_END = A.o
        assert PERSIST_END <= 40 * KB, PERSIST_END
        MIX_OFF = 40 * KB
        mixT = R(MIX_OFF, [128, 8, S], BF16)
        PH = 72 * KB

        NEGLAM = small[:, 0:1]
        SGN = small[:, 1:2]

        P = Prog()
        add = P.add

        add("sp", lambda h: h.dma_start(out=ctf[:], in_=ctab), writes=["ctf"], dma_key="c0")
        add("sp", lambda h: h.dma_start(out=lamb[:], in_=lamv.partition_broadcast(128)), writes=["lamb"], dma_key="c1")
        add("sp", lambda h: h.dma_start(out=sgbc[:], in_=subg.partition_broadcast(128)), writes=["sgbc"], dma_key="c2")
        for cc in range(4):
            add("sp", lambda h, cc=cc: h.dma_start(out=convw[:, cc, :], in_=conv_w[:, cc * 128:(cc + 1) * 128].rearrange("r p -> p r"),
                                                   allow_slow_non_contiguous=True), writes=[("convw", cc)], dma_key="c3")
        add("sp", lambda h: h.dma_start(out=ptb[:], in_=ptab.partition_broadcast(128)), writes=["ptb"], dma_key="c4")
        add("sp", lambda h: h.dma_start(out=gbc[:], in_=n1g.partition_broadcast(128)), writes=["gbc"], dma_key="c5")
        add("pool", lambda h: h.iota(idx[:, 0:1], [[0, 1]], base=0, channel_multiplier=1), writes=["iot0"])
        add("dve", lambda h: h.tensor_copy(out=iot[:, 0:1], in_=idx[:, 0:1]), reads=["iot0"], writes=["iot"])
        add("dve", lambda h: h.tensor_scalar(out=idx[:], in0=ptb[:], scalar1=128.0, scalar2=iot[:, 0:1],
                                             op0=ALU.mult, op1=ALU.add), reads=["ptb", "iot", "iot0"], writes=["idx"])
        add("dve", lambda h: h.tensor_copy(out=identb[:], in_=ctf[:, C_IDENT:C_IDENT + 128]), reads=["ctf"], writes=["identb"])
        add("dve", lambda h: h.tensor_copy(out=maskb[:], in_=ctf[:, C_MASK:C_MASK + 128]), reads=["ctf"], writes=["maskb"])
        add("dve", lambda h: h.tensor_copy(out=auxb[:], in_=ctf[:, C_AUXR:C_AUXR + 1536]), reads=["ctf"], writes=["auxb"])
        add("dve", lambda h: h.tensor_copy(out=newbb[:], in_=ctf[:, C_NEWB:C_NEWB + 8]), reads=["ctf"], writes=["newbb"])
        add("dve", lambda h: h.tensor_scalar(out=sgbc[:], in0=sgbc[:], scalar1=1.0 - LAM_INIT, scalar2=None, op0=ALU.mult),
            reads=["sgbc"], writes=["sgbc"])
        add("dve", lambda h: h.tensor_tensor(out=lamb[:, 0:64], in0=lamb[:, 0:64], in1=lamb[:, 64:128], op=ALU.mult),
            reads=["lamb"], writes=["lamb"])
        add("dve", lambda h: h.tensor_tensor(out=lamb[:, 128:192], in0=lamb[:, 128:192], in1=lamb[:, 192:256], op=ALU.mult),
            reads=["lamb"], writes=["lamb"])
        add("dve", lambda h: h.reduce_sum(out=small[:, 2:3], in_=lamb[:, 0:64], axis=mybir.AxisListType.X),
            reads=["lamb"], writes=["s2"])
        add("dve", lambda h: h.reduce_sum(out=small[:, 3:4], in_=lamb[:, 128:192], axis=mybir.AxisListType.X),
            reads=["lamb"], writes=["s3"])
        add("act", lambda h: h.activation(out=small[:, 2:4], in_=small[:, 2:4], func=AF.Exp), reads=["s2", "s3"], writes=["s2", "s3"])
        add("dve", lambda h: h.tensor_tensor(out=small[:, 0:1], in0=small[:, 3:4], in1=small[:, 2:3], op=ALU.subtract),
            reads=["s2", "s3"], writes=["neglam"])
        add("dve", lambda h: h.tensor_scalar(out=small[:, 0:1], in0=small[:, 0:1], scalar1=-LAM_INIT, scalar2=None, op0=ALU.add),
            reads=["neglam"], writes=["neglam"])
        add("dve", lambda h: h.scalar_tensor_tensor(out=small[0:16, 1:2], in0=ctf[0:16, C_M2:C_M2 + 1], scalar=small[0:16, 0:1],
                                                    in1=ctf[0:16, C_M1:C_M1 + 1], op0=ALU.mult, op1=ALU.add),
            reads=["neglam", "ctf"], writes=["sgn"])

        psrr = [0]

        def nextps():
            i = psrr[0] % 8
            psrr[0] += 1
            return i

        def rstd_ops(ssap, n, key, npart=128):
            add("dve", lambda h: h.tensor_scalar(out=ssap, in0=ssap, scalar1=1.0 / n, scalar2=EPS, op0=ALU.mult, op1=ALU.add),
                reads=[key], writes=[key])
            add("act", lambda h: h.activation(out=ssap, in_=ssap, func=AF.Ln), reads=[key], writes=[key])
            add("act", lambda h: h.activation(out=ssap, in_=ssap, func=AF.Exp, scale=-0.5), reads=[key], writes=[key])

        B1 = Alloc(PH)
        hT = B1([128, 8, 1024], BF16)
        Wsl = [B1([128, 8, 512], BF16) for _ in range(4)]
        KT = B1([128, 4, S], BF16)
        QT = B1([128, 4, S], BF16)
        Vaug = B1([128, 16, 4, 130], BF16)
        CU_OFF = B1.o
        cu = B1([128, 4, 514], F32)
        csb = B1([128, 512], F32)
        ct1 = B1([128, 512], F32)
        ct2 = B1([128, 512], F32)
        _pad = [B1([128, 512], F32) for _ in range(2)]
        kvst = [R(MIX_OFF + 16 * KB + i * 2048, [128, 512], F32) for i in range(2)]
        xin = [B1([128, 1024], F32) for _ in range(2)]
        hb = [B1([128, 1024], BF16) for _ in range(2)]
        ssq = B1([128, 32], F32)
        P1_END = B1.o
        assert P1_END <= ARENA_B, P1_END

        add("pool", lambda h: h.memset(Vaug[:, :, :, 128:129], 1.0), writes=[("V1",)])
        add("pool", lambda h: h.memset(vaugn[:, :, :, 128:129], 1.0), writes=[("Vn1",)])
        add("pool", lambda h: h.memset(cu[:, :, 0:2], 0.0), writes=[("cuh", c_) for c_ in range(4)])
        add("pool", lambda h: h.memset(qbd[:], 0.0), writes=["qbd"])

        wcount = [0]

        def load_w(src_ap, ncols_ok=True):
            sl = wcount[0] % 4
            wcount[0] += 1
            add("pool", lambda h: h.dma_start(out=Wsl[sl][:], in_=src_ap), writes=[("W", sl)], dma_key=("W", sl))
            return sl

        def win_block(col0):
            return w_in[:, col0:col0 + 512].rearrange("(k p) n -> p k n", p=128)

        def norm_tile(src_ap, slot, npart, gkey="gbc"):
            add("sp", lambda h: h.dma_start(out=xin[slot][:npart], in_=src_ap), writes=[("xin", slot)], dma_key=("xin", slot))
            sa = ssq[:npart, slot:slot + 1]
            add("act", lambda h: h.activation(out=hb[slot][:npart], in_=xin[slot][:npart], func=AF.Square, accum_out=sa),
                reads=[("xin", slot)], writes=[("hb", slot), ("ss", slot)])
            rstd_ops(sa, D, ("ss", slot))
            add("dve", lambda h: h.scalar_tensor_tensor(out=hb[slot][:npart], in0=xin[slot][:npart], scalar=sa,
                                                        in1=gbc[:npart], op0=ALU.mult, op1=ALU.mult),
                reads=[("xin", slot), ("ss", slot), gkey], writes=[("hb", slot)])

        def transpose_to(slot, npart, dst_ap, dkeys, use_act):
            pi = nextps()
            tp = psb[pi][:].bitcast(BF16)[:, 0:8 * 128].rearrange("p (k n) -> p k n", k=8)
            for kc in range(8):
                add("pe", lambda h, kc=kc: h.transpose(out=tp[:, kc, 0:npart], in_=hb[slot][:npart, kc * 128:(kc + 1) * 128],
                                                       identity=identb[:npart, :npart]),
                    reads=[("hb", slot), "identb"], writes=[("ps", pi)])
            if use_act:
                add("act", lambda h: h.copy(out=dst_ap, in_=tp[:, :, 0:npart]), reads=[("ps", pi)], writes=dkeys)
            else:
                add("dve", lambda h: h.tensor_copy(out=dst_ap, in_=tp[:, :, 0:npart]), reads=[("ps", pi)], writes=dkeys)

        norm_tile(xs, 0, 32)
        transpose_to(0, 32, hTs[:], ["hTs"], True)

        def proj_fm(wsl, hh, rhs_ap, n, rkeys):
            pi = nextps()
            for kc in range(8):
                add("pe", lambda h, kc=kc: h.matmul(psb[pi][:, 0:n], lhsT=Wsl[wsl][:, kc, hh * 128:(hh + 1) * 128],
                                                    rhs=rhs_ap(kc), start=(kc == 0), stop=(kc == 7)),
                    reads=[("W", wsl)] + rkeys, writes=[("ps", pi)])
            return pi

        def proj_tm(wsl, lhs_ap, m, rkeys):
            pi = nextps()
            for kc in range(8):
                add("pe", lambda h, kc=kc: h.matmul(psb[pi][0:m, :], lhsT=lhs_ap(kc), rhs=Wsl[wsl][:, kc, :],
                                                    start=(kc == 0), stop=(kc == 7)),
                    reads=[("W", wsl)] + rkeys, writes=[("ps", pi)])
            return pi

        stc = [0]

        for hs in range(2):
            for t in range(8):
                gt = hs * 8 + t
                slot = gt % 2
                norm_tile(xp[gt * 128:(gt + 1) * 128, :], slot, 128)
                transpose_to(slot, 128, hT[:, :, t * 128:(t + 1) * 128], [("hT", t)], t % 2 == 0)
            hkeys = [("hT", t) for t in range(8)]

            wk = load_w(win_block(1536 + 512))
            for cj in range(2):
                gj = hs * 2 + cj
                for hh in range(4):
                    pi = proj_fm(wk, hh, lambda kc, cj=cj: hT[:, kc, cj * 512:(cj + 1) * 512], 512, hkeys[cj * 4:cj * 4 + 4])
                    add("act", lambda h, pi=pi, hh=hh, gj=gj: h.copy(out=KT[:, hh, gj * 512:(gj + 1) * 512], in_=psb[pi][:]),
                        reads=[("ps", pi)], writes=[("KT", gj, hh)])
            for t in range(8):
                gt = hs * 8 + t
                pi = proj_tm(wk, lambda kc, t=t: hT[:, kc, t * 128:(t + 1) * 128], 128, [("hT", t)])
                st = stc[0] % 2
                stc[0] += 1
                if _os.environ.get("KVACT") == "2":
                    add("act", lambda h, pi=pi, st=st: h.copy(out=csb[:], in_=psb[pi][:]), reads=[("ps", pi)], writes=[("kvst", st)])
                elif _os.environ.get("KVACT") == "3":
                    add("act", lambda h, pi=pi, st=st: h.copy(out=kvst[st][:], in_=hT[:, 0, 0:512]), reads=[("ps", pi)], writes=[("kvst", st)])
                elif _os.environ.get("KVACT"):
                    add("act", lambda h, pi=pi, st=st: h.copy(out=kvst[st][:], in_=psb[pi][:]), reads=[("ps", pi)], writes=[("kvst", st)])
                else:
                    add("dve", lambda h, pi=pi, st=st: h.tensor_copy(out=kvst[st][:], in_=psb[pi][:]), reads=[("ps", pi)], writes=[("kvst", st)])
                add("sp", lambda h, st=st, gt=gt: h.dma_start(out=kp[gt * 128:(gt + 1) * 128, :], in_=kvst[st][:]),
                    reads=[("kvst", st)], dma_key=("kvst", st))
            if hs == 1:
                for hh in range(4):
                    pi = proj_fm(wk, hh, lambda kc: hTs[:, kc, :], 32, ["hTs"])
                    add("act", lambda h, pi=pi, hh=hh: h.copy(out=kTn[:, hh, :], in_=psb[pi][:, 0:32]), reads=[("ps", pi)], writes=["kTn"])
                pi = proj_tm(wk, lambda kc: hTs[:, kc, :], 32, ["hTs"])
                st = stc[0] % 2
                stc[0] += 1
                add("dve", lambda h, pi=pi, st=st: h.tensor_copy(out=kvst[st][0:32], in_=psb[pi][0:32, :]), reads=[("ps", pi)], writes=[("kvst", st)])
                add("sp", lambda h, st=st: h.dma_start(out=ks, in_=kvst[st][0:32]), reads=[("kvst", st)], dma_key=("kvst", st))

            wv = load_w(win_block(1536 + 1024))
            for t in range(8):
                gt = hs * 8 + t
                pi = proj_tm(wv, lambda kc, t=t: hT[:, kc, t * 128:(t + 1) * 128], 128, [("hT", t)])
                st = stc[0] % 2
                stc[0] += 1
                add("dve", lambda h, pi=pi, st=st: h.tensor_copy(out=kvst[st][:], in_=psb[pi][:]), reads=[("ps", pi)], writes=[("kvst", st)])
                for h4 in range(4):
                    add("act", lambda h, st=st, gt=gt, h4=h4: h.copy(out=Vaug[:, gt, h4, 0:128], in_=kvst[st][:, h4 * 128:(h4 + 1) * 128]),
                        reads=[("kvst", st)], writes=[("V", gt, h4)])
                add("sp", lambda h, st=st, gt=gt: h.dma_start(out=vp[gt * 128:(gt + 1) * 128, :], in_=kvst[st][:]),
                    reads=[("kvst", st)], dma_key=("kvst", st))
            if hs == 1:
                for b in range(4):
                    pi = proj_tm(wv, lambda kc, b=b: hTs[:, kc, b * 8:(b + 1) * 8], 8, ["hTs"])
                    st = stc[0] % 2
                    stc[0] += 1
                    add("dve", lambda h, pi=pi, st=st: h.tensor_copy(out=kvst[st][0:8], in_=psb[pi][0:8, :]), reads=[("ps", pi)], writes=[("kvst", st)])
                    for h4 in range(4):
                        add("act", lambda h, st=st, b=b, h4=h4: h.copy(out=vaugn[0:8, b, h4, 0:128], in_=kvst[st][0:8, h4 * 128:(h4 + 1) * 128]),
                            reads=[("kvst", st)], writes=[("Vn", b)])
                    add("sp", lambda h, st=st, b=b: h.dma_start(out=vs[b * 8:(b + 1) * 8, :], in_=kvst[st][0:8]),
                        reads=[("kvst", st)], dma_key=("kvst", st))

            wq = load_w(win_block(1536))
            for cj in range(2):
                gj = hs * 2 + cj
                for hh in range(4):
                    pi = proj_fm(wq, hh, lambda kc, cj=cj: hT[:, kc, cj * 512:(cj + 1) * 512], 512, hkeys[cj * 4:cj * 4 + 4])
                    add("act", lambda h, pi=pi, hh=hh, gj=gj: h.copy(out=QT[:, hh, gj * 512:(gj + 1) * 512], in_=psb[pi][:]),
                        reads=[("ps", pi)], writes=[("QT", gj, hh)])
            if hs == 1:
                for hh in range(4):
                    pi = proj_fm(wq, hh, lambda kc: hTs[:, kc, :], 32, ["hTs"])
                    for c in range(2):
                        add("act", lambda h, pi=pi, hh=hh, c=c: h.activation(
                            out=qbd[c * 64:(c + 1) * 64, :, hh, hh * 16 + c * 8: hh * 16 + c * 8 + 8],
                            in_=psb[pi][c * 64:(c + 1) * 64, 0:32].rearrange("p (b q) -> p b q", b=4),
                            func=AF.Copy, scale=0.125), reads=[("ps", pi)], writes=["qbd"])

            wB = load_w(win_block(0))
            wC = load_w(win_block(512))
            wU = load_w(win_block(1024))
            for cj in range(2):
                gj = hs * 2 + cj
                for cc in range(4):
                    rk = hkeys[cj * 4:cj * 4 + 4]
                    rhs = lambda kc, cj=cj: hT[:, kc, cj * 512:(cj + 1) * 512]
                    pC = proj_fm(wC, cc, rhs, 512, rk)
                    pU = proj_fm(wU, cc, rhs, 512, rk)
                    pB = proj_fm(wB, cc, rhs, 512, rk)
                    add("act", lambda h, pC=pC: h.copy(out=csb[:], in_=psb[pC][:]), reads=[("ps", pC)], writes=["csb"])
                    if gj > 0:
                        add("dve", lambda h, cc=cc: h.tensor_copy(out=cu[:, cc, 0:2], in_=cu[:, cc, 512:514]),
                            reads=[("cu", cc)], writes=[("cuh", cc)])
                    add("dve", lambda h, cc=cc, pU=pU: h.tensor_tensor(out=cu[:, cc, 2:514], in0=psb[pU][:], in1=csb[:], op=ALU.mult),
                        reads=[("ps", pU), "csb", ("cuh", cc)], writes=[("cu", cc)])
                    add("dve", lambda h, cc=cc: h.tensor_scalar(out=ct1[:], in0=cu[:, cc, 0:512], scalar1=convw[:, cc, 0:1], scalar2=None, op0=ALU.mult),
                        reads=[("cu", cc), ("cuh", cc), ("convw", 0), ("convw", 1), ("convw", 2), ("convw", 3)], writes=["ct1"])
                    add("dve", lambda h, cc=cc: h.scalar_tensor_tensor(out=ct2[:], in0=cu[:, cc, 1:513], scalar=convw[:, cc, 1:2], in1=ct1[:],
                                                                       op0=ALU.mult, op1=ALU.add),
                        reads=[("cu", cc), ("cuh", cc), "ct1", ("convw", 0), ("convw", 1), ("convw", 2), ("convw", 3)], writes=["ct2"])
                    add("dve", lambda h, cc=cc: h.scalar_tensor_tensor(out=ct1[:], in0=cu[:, cc, 2:514], scalar=convw[:, cc, 2:3], in1=ct2[:],
                                                                       op0=ALU.mult, op1=ALU.add),
                        reads=[("cu", cc), "ct2", ("convw", 0), ("convw", 1), ("convw", 2), ("convw", 3)], writes=["ct1"])
                    add("dve", lambda h, cc=cc, pB=pB, gj=gj: h.tensor_tensor(out=mixT[:, cc, gj * 512:(gj + 1) * 512], in0=ct1[:], in1=psb[pB][:], op=ALU.mult),
                        reads=["ct1", ("ps", pB)], writes=[("mixc", cc, gj)])
            if hs == 1:
                for cc in range(4):
                    add("sp", lambda h, cc=cc: h.dma_start(out=cp[:, cc * 128:(cc + 1) * 128].rearrange("r p -> p r"), in_=cu[:, cc, 512:514],
                                                           allow_slow_non_contiguous=True),
                        reads=[("cu", cc)], dma_key="cp")
                for cc in range(4):
                    for b in range(4):
                        add("sp", lambda h, cc=cc, b=b: h.dma_start(out=cus[:, cc, b, 0:2], in_=sconv[b, :, cc * 128:(cc + 1) * 128].rearrange("r p -> p r"),
                                                                    allow_slow_non_contiguous=True), writes=[("cus_h", cc, b)], dma_key=("cus_h", cc))
                for cc in range(4):
                    rhs = lambda kc: hTs[:, kc, :]
                    pC = proj_fm(wC, cc, rhs, 32, ["hTs"])
                    pU = proj_fm(wU, cc, rhs, 32, ["hTs"])
                    pB = proj_fm(wB, cc, rhs, 32, ["hTs"])
                    v3 = lambda ap: ap.rearrange("p (b q) -> p b q", b=4)
                    add("act", lambda h, pC=pC: h.copy(out=csb[:, 0:32], in_=psb[pC][:, 0:32]), reads=[("ps", pC)], writes=["csb"])
                    add("dve", lambda h, cc=cc, pU=pU: h.tensor_tensor(out=cus[:, cc, :, 2:10], in0=v3(psb[pU][:, 0:32]), in1=v3(csb[:, 0:32]), op=ALU.mult),
                        reads=[("ps", pU), "csb"], writes=[("cus", cc)])
                    add("dve", lambda h, cc=cc: h.tensor_scalar(out=v3(ct1[:, 0:32]), in0=cus[:, cc, :, 0:8], scalar1=convw[:, cc, 0:1], scalar2=None, op0=ALU.mult),
                        reads=[("cus", cc)] + [("cus_h", cc, b_) for b_ in range(4)] + [("convw", 0), ("convw", 1), ("convw", 2), ("convw", 3)], writes=["ct1"])
                    add("dve", lambda h, cc=cc: h.scalar_tensor_tensor(out=v3(ct2[:, 0:32]), in0=cus[:, cc, :, 1:9], scalar=convw[:, cc, 1:2], in1=v3(ct1[:, 0:32]),
                                                                       op0=ALU.mult, op1=ALU.add),
                        reads=[("cus", cc), "ct1"] + [("cus_h", cc, b_) for b_ in range(4)], writes=["ct2"])
                    add("dve", lambda h, cc=cc: h.scalar_tensor_tensor(out=v3(ct1[:, 0:32]), in0=cus[:, cc, :, 2:10], scalar=convw[:, cc, 2:3], in1=v3(ct2[:, 0:32]),
                                                                       op0=ALU.mult, op1=ALU.add),
                        reads=[("cus", cc), "ct2"], writes=["ct1"])
                    add("dve", lambda h, cc=cc, pB=pB: h.tensor_tensor(out=mixTs[:, cc, :], in0=ct1[:, 0:32], in1=psb[pB][:, 0:32], op=ALU.mult),
                        reads=["ct1", ("ps", pB)], writes=[("mixcs", cc)])
                for cc in range(4):
                    for b in range(4):
                        add("sp", lambda h, cc=cc, b=b: h.dma_start(out=cs[b, :, cc * 128:(cc + 1) * 128].rearrange("r p -> p r"), in_=cus[:, cc, b, 8:10],
                                                                    allow_slow_non_contiguous=True),
                            reads=[("cus", cc)], dma_key="cs")

        if stage < 2:
            P.emit(nc, es)
            return nc
        ov_keys = [("hT", t) for t in range(8)] + [("W", s_) for s_ in range(4)] + [("kvst", 0), ("kvst", 1)]
        ov_keys += ([("cu", c_) for c_ in range(4)] + [("cuh", c_) for c_ in range(4)] + ["csb", "ct1", "ct2"]
                    + [("xin", 0), ("xin", 1), ("hb", 0), ("hb", 1), ("ss", 0), ("ss", 1)])
        B2 = Alloc(PH)
        PT = [B2([128, 2, 512], BF16) for _ in range(3)]
        obuf = B2([128, 4, 512], F32)
        obf = [B2([128, 512], BF16) for _ in range(2)]
        tbuf = [B2([128, 128], F32) for _ in range(2)]
        rl = B2([128, 64], F32)
        ssa = B2([128, 16], F32)
        zer = B2([128, 512], BF16)
        NKV = 16
        BK = Alloc(CU_OFF)
        kvp = [BK([128, 1028], BF16) for _ in range(NKV)]
        assert BK.o <= ARENA_B, BK.o
        Psn = B2([128, 8], BF16)
        PTn = B2([128, 64], BF16)
        KTs = [B2([128, 4, 512], BF16) for _ in range(2)]
        Psb = [B2([128, 512], BF16) for _ in range(2)]
        PTs = [B2([128, 4, 64], BF16) for _ in range(2)]
        asb = B2([128, 512], F32)
        osb = B2([128, 512], F32)
        osbb = B2([128, 512], BF16)
        asbb = B2([128, 512], BF16)
        selb = B2([128, 8], BF16)
        assert B2.o <= PH + 48 * KB, B2.o

        first_ov = [True]

        def ovw(keys):
            if first_ov[0]:
                first_ov[0] = False
                return list(keys) + ov_keys
            return list(keys)

        add("pool", lambda h: h.memset(zer[:], 0.0), writes=ovw(["zer"]) + [("PT", i, c_, b_) for i in range(3) for c_ in range(2) for b_ in range(4)]
            + [("obuf", m_, h_) for m_ in range(4) for h_ in range(4)] + [("obf", o_, h_) for o_ in range(2) for h_ in range(4)]
            + [("tbuf", 0), ("tbuf", 1)] + [("rl", h_, m_, i_) for h_ in range(4) for m_ in range(4) for i_ in range(2)]
            + [("ssa", m_, h_) for m_ in range(4) for h_ in range(4)] + [("rls", h_) for h_ in range(4)] + ["rls2"] + [("asb", h_) for h_ in range(4)]
            + [("kvp", i) for i in range(16)] + [("KTs", i, jj_) for i in range(2) for jj_ in range(4)]
            + [("Psb", i) for i in range(2)] + [("PTs", i) for i in range(2)] + ["osb", "osbb", "Psn", "PTn"] + [("accs", i) for i in range(3)])
        for i in range(NKV):
            add("pool", lambda h, i=i: h.memset(kvp[i][:, 1024:1028], 1.0), reads=["zer"], writes=[("kvp1", i)])
        add("pool", lambda h: h.memset(asbb[:], 0.0), reads=["zer"], writes=["asbb"])
        add("dve", lambda h: h.tensor_copy(out=selb[:], in_=ctf[:, C_SEL:C_SEL + 8]), reads=["zer", "ctf"], writes=["selb"])

        ACC = [5, 6, 7]
        SB = [(0, 1), (2, 3)]
        TPB = 4

        def acc_ap(sl, npart=128):
            b, pos = divmod(sl, 3)
            return psb[ACC[b]][0:npart, pos * 168: pos * 168 + 129]

        ptc = [0]
        sbc = [0]
        tcnt = [0]
        accs = B2([128, 3, 504], F32)
        assert B2.o <= PH + 48 * KB, B2.o
        units = [(j, hh, kt) for j in range(0 if 'p2' not in skip else 4, 4) for hh in range(4) for kt in range(4 * j + 4)]
        uinfo = {}

        def stS(u):
            j, hh, kt = units[u]
            m0 = max(0, kt - 4 * j)
            qs = m0 * 128
            sp_ = SB[u % 2]
            pt = u % 3
            uinfo[u] = (pt, m0)
            for c in range(2):
                add("pe", lambda h, c=c, sp_=sp_, kt=kt, hh=hh, j=j, qs=qs: h.matmul(
                    psb[sp_[c]][:, qs:512], lhsT=KT[c * 64:(c + 1) * 64, hh, kt * 128:(kt + 1) * 128],
                    rhs=QT[c * 64:(c + 1) * 64, hh, j * 512 + qs:(j + 1) * 512], start=True, stop=True),
                    reads=[("KT", kt // 4, hh), ("QT", j, hh)], writes=[("ps", sp_[c])])
            step = 128 if hh == 0 else 512
            q0 = qs
            while q0 < 512:
                q1 = min(512, q0 + step)
                d_ = ((4 * j + q0 // 128) - kt) if hh == 0 else (4 * j - kt)
                for c in range(2):
                    add("act", lambda h, c=c, sp_=sp_, pt=pt, q0=q0, q1=q1, d_=d_, hh=hh: h.activation(
                        out=PT[pt][:, c, q0:q1], in_=psb[sp_[c]][:, q0:q1], func=AF.Exp,
                        bias=ctf[:, C_TABP2 + hh * 19 + d_ + 3: C_TABP2 + hh * 19 + d_ + 4], scale=0.125),
                        reads=[("ps", sp_[c]), "ctf"], writes=[("PT", pt, c, b_) for b_ in range(q0 // 128, (q1 + 127) // 128)])
                q0 = q1
            if kt >= 4 * j:
                for c in range(2):
                    add("dve", lambda h, pt=pt, qs=qs, c=c: h.tensor_tensor(
                        out=PT[pt][:, c, qs:qs + 128], in0=PT[pt][:, c, qs:qs + 128], in1=maskb[:], op=ALU.mult),
                        reads=["maskb"], writes=[("PT", pt, c, qs // 128)])

        def accs_ap(sl):
            bnk, pos = divmod(sl, 3)
            return accs[:, bnk, pos * 168: pos * 168 + 129]

        def stPV(u):
            j, hh, kt = units[u]
            pt, m0 = uinfo[u]
            if kt == 0:
                for b in ACC:
                    add("pe", lambda h, b=b: h.matmul(psb[b][:], lhsT=zer[:, 0:128], rhs=zer[:], start=True, stop=True),
                        reads=["zer"], writes=[("ps", b)])
            for m in range(m0, 4):
                for c in range(2):
                    sl = 2 * m + c
                    add("pe", lambda h, sl=sl, pt=pt, c=c, m=m, kt=kt, hh=hh: h.matmul(
                        acc_ap(sl), lhsT=PT[pt][:, c, m * 128:(m + 1) * 128], rhs=Vaug[:, kt, hh, 0:129],
                        start=False, stop=(kt == 4 * j + m), skip_group_check=True),
                        reads=[("PT", pt, c, m), ("V", kt, hh), ("V1",)], writes=[("ps", ACC[sl // 3])])
            if kt != 4 * j + 3:
                return
            for bi, b in enumerate(ACC):
                add("dve", lambda h, bi=bi, b=b: h.tensor_copy(out=accs[:, bi, :], in_=psb[b][:, 0:504]), reads=[("ps", b)], writes=[("accs", bi)])
            for m in range(4):
                a0 = accs_ap(2 * m)
                a1 = accs_ap(2 * m + 1)
                k0 = ("accs", (2 * m) // 3)
                k1 = ("accs", (2 * m + 1) // 3)
                r0 = rl[:, (hh * 4 + m) * 2:(hh * 4 + m) * 2 + 1]
                r1 = rl[:, (hh * 4 + m) * 2 + 1:(hh * 4 + m) * 2 + 2]
                tb = tcnt[0] % 2
                tcnt[0] += 1
                add("dve", lambda h, a0=a0, r0=r0: h.reciprocal(out=r0, in_=a0[:, 128:129]), reads=[k0], writes=[("rl", hh, m, 0)])
                add("dve", lambda h, a1=a1, r1=r1: h.reciprocal(out=r1, in_=a1[:, 128:129]), reads=[k1], writes=[("rl", hh, m, 1)])
                add("dve", lambda h, r1=r1: h.tensor_tensor(out=r1, in0=r1, in1=NEGLAM, op=ALU.mult), reads=["neglam"], writes=[("rl", hh, m, 1)])
                add("act", lambda h, a0=a0, r0=r0, tb=tb: h.activation(out=tbuf[tb][:], in_=a0[:, 0:128], func=AF.Copy, scale=r0),
                    reads=[k0, ("rl", hh, m, 0)], writes=[("tbuf", tb)])
                add("dve", lambda h, a1=a1, r1=r1, tb=tb, m=m, hh=hh: h.scalar_tensor_tensor(
                    out=obuf[:, m, hh * 128:(hh + 1) * 128], in0=a1[:, 0:128], scalar=r1, in1=tbuf[tb][:], op0=ALU.mult, op1=ALU.add),
                    reads=[k1, ("rl", hh, m, 1), ("tbuf", tb)], writes=[("obuf", m, hh)])
                add("act", lambda h, tb=tb, m=m, hh=hh: h.activation(out=tbuf[tb][:], in_=obuf[:, m, hh * 128:(hh + 1) * 128], func=AF.Square,
                                                                     accum_out=ssa[:, m * 4 + hh: m * 4 + hh + 1]),
                    reads=[("obuf", m, hh)], writes=[("tbuf", tb), ("ssa", m, hh)])
            if hh != 3:
                return
            ssk = [("ssa", m_, h_) for m_ in range(4) for h_ in range(4)]
            add("dve", lambda h: h.tensor_scalar(out=ssa[:, 0:16], in0=ssa[:, 0:16], scalar1=1.0 / 128, scalar2=EPS, op0=ALU.mult, op1=ALU.add),
                reads=ssk, writes=ssk)
            add("act", lambda h: h.activation(out=ssa[:, 0:16], in_=ssa[:, 0:16], func=AF.Ln), reads=ssk, writes=ssk)
            add("act", lambda h: h.activation(out=ssa[:, 0:16], in_=ssa[:, 0:16], func=AF.Exp, scale=-0.5), reads=ssk, writes=ssk)
            for m in range(4):
                gt = 4 * j + m
                ob = gt % 2
                for h2 in range(4):
                    add("dve", lambda h, m=m, h2=h2, ob=ob: h.scalar_tensor_tensor(
                        out=obf[ob][:, h2 * 128:(h2 + 1) * 128], in0=obuf[:, m, h2 * 128:(h2 + 1) * 128],
                        scalar=ssa[:, m * 4 + h2: m * 4 + h2 + 1], in1=sgbc[:], op0=ALU.mult, op1=ALU.mult),
                        reads=[("obuf", m, h2), ("ssa", m, h2), "sgbc"], writes=[("obf", ob, h2)])
                tp = psb[TPB][:].bitcast(BF16)[:, 0:512].rearrange("p (k n) -> p k n", k=4)
                for h2 in range(4):
                    add("pe", lambda h, h2=h2, ob=ob: h.transpose(out=tp[:, h2, :], in_=obf[ob][:, h2 * 128:(h2 + 1) * 128], identity=identb[:]),
                        reads=[("obf", ob, h2), "identb"], writes=[("ps", TPB)])
                add("act", lambda h, gt=gt: h.copy(out=mixT[:, 4:8, gt * 128:(gt + 1) * 128], in_=tp), reads=[("ps", TPB)], writes=[("mixa", gt)])

        if units:
            stS(0)
            for u in range(len(units)):
                if u + 1 < len(units):
                    stS(u + 1)
                stPV(u)

        SACC = [5, 6]
        SS = [0, 1]
        TK = [2, 3]
        TPP = 4
        OPS = 7

        def sacc_ap(hh):
            return psb[SACC[hh // 2]][0:16, (hh % 2) * 168:(hh % 2) * 168 + 129]

        NG = 64 if 'p2s' not in skip else 0
        vslots = {}

        def stG(i):
            b_, g = divmod(i, 16)
            vsl = []
            for jj in range(4):
                pg = b_ * 64 + 4 * g + jj
                ksl = (4 * i + jj) % NKV
                vsl.append(ksl)
                add("pool", lambda h, ksl=ksl, pg=pg: h.indirect_dma_start(
                    out=kvp[ksl][:, 0:1024], out_offset=None, in_=ckv, in_offset=bass.IndirectOffsetOnAxis(ap=idx[:, pg:pg + 1], axis=0)),
                    reads=["idx"], writes=[("kvp", ksl)], dma_key=("kvp", ksl))
            vslots[i] = vsl

        def stA(i):
            gs = i % 2
            for jj in range(4):
                ksl = vslots[i][jj]
                tki = TK[jj % 2]
                tpk = psb[tki][:].bitcast(BF16)[:, 0:512].rearrange("p (k n) -> p k n", k=4)
                for hh in range(4):
                    add("pe", lambda h, hh=hh, ksl=ksl, tpk=tpk: h.transpose(out=tpk[:, hh, :], in_=kvp[ksl][:, hh * 128:(hh + 1) * 128], identity=identb[:]),
                        reads=[("kvp", ksl), "identb"], writes=[("ps", tki)])
                if jj % 2 == 0:
                    add("act", lambda h, gs=gs, jj=jj, tpk=tpk: h.copy(out=KTs[gs][:, :, jj * 128:(jj + 1) * 128], in_=tpk),
                        reads=[("ps", tki)], writes=[("KTs", gs, jj)])
                else:
                    add("dve", lambda h, gs=gs, jj=jj, tpk=tpk: h.tensor_copy(out=KTs[gs][:, :, jj * 128:(jj + 1) * 128], in_=tpk),
                        reads=[("ps", tki)], writes=[("KTs", gs, jj)])

        def stB(i):
            b_, g = divmod(i, 16)
            gs = i % 2
            si = SS[i % 2]
            for hh in range(4):
                add("pe", lambda h, hh=hh, si=si, gs=gs, b_=b_: h.matmul(psb[si][0:64, :], lhsT=qbd[:, b_, hh, :], rhs=KTs[gs][:, hh, :],
                                                                          start=(hh == 0), stop=False),
                    reads=["qbd"] + [("KTs", gs, jj_) for jj_ in range(4)], writes=[("ps", si)])
            add("pe", lambda h, si=si, g=g: h.matmul(psb[si][0:64, :], lhsT=auxb[:, 512 + g * 64: 512 + (g + 1) * 64], rhs=auxb[:, 0:512],
                                                     start=False, stop=True),
                reads=["auxb"], writes=[("ps", si)])
            add("act", lambda h, si=si, gs=gs: h.activation(out=Psb[gs][0:64, :], in_=psb[si][0:64, :], func=AF.Exp),
                reads=[("ps", si)], writes=[("Psb", gs)])

        def stC(i):
            gs = i % 2
            tpp = psb[TPP][:].bitcast(BF16)[:, 0:256].rearrange("p (k n) -> p k n", k=4)
            for jj in range(4):
                add("pe", lambda h, jj=jj, gs=gs: h.transpose(out=tpp[:, jj, :], in_=Psb[gs][0:64, jj * 128:(jj + 1) * 128], identity=identb[0:64, 0:64]),
                    reads=[("Psb", gs), "identb"], writes=[("ps", TPP)])
            add("dve", lambda h, gs=gs: h.tensor_copy(out=PTs[gs][:], in_=tpp), reads=[("ps", TPP)], writes=[("PTs", gs)])

        def stD(i):
            gs = i % 2
            vsl = vslots[i]
            for jj in range(4):
                for hh in range(4):
                    v_ = vsl[jj]
                    add("pe", lambda h, jj=jj, hh=hh, gs=gs, v_=v_: h.matmul(
                        sacc_ap(hh)[:, 0:128], lhsT=PTs[gs][:, jj, hh * 16:(hh + 1) * 16], rhs=kvp[v_][:, 512 + hh * 128:512 + (hh + 1) * 128], start=False, stop=False,
                        skip_group_check=True),
                        reads=[("PTs", gs), ("kvp", v_)], writes=[("ps", SACC[hh // 2])])
                    add("pe", lambda h, jj=jj, hh=hh, gs=gs, v_=v_: h.matmul(
                        sacc_ap(hh)[:, 128:129], lhsT=PTs[gs][:, jj, hh * 16:(hh + 1) * 16], rhs=kvp[v_][:, 1024:1025], start=False, stop=False,
                        skip_group_check=True),
                        reads=[("PTs", gs), ("kvp", v_), ("kvp1", v_)], writes=[("ps", SACC[hh // 2])])

        def batch_init(b_):
            for bk in SACC:
                add("pe", lambda h, bk=bk: h.matmul(psb[bk][:], lhsT=zer[:, 0:128], rhs=zer[:], start=True, stop=True),
                    reads=["zer"], writes=[("ps", bk)])

        def batch_fin(b):
            si = OPS
            for hh in range(4):
                add("pe", lambda h, hh=hh, b=b: h.matmul(psb[si][0:64, 0:8], lhsT=qbd[:, b, hh, :], rhs=kTn[:, hh, b * 8:(b + 1) * 8],
                                                          start=(hh == 0), stop=False),
                    reads=["qbd", "kTn"], writes=[("ps", si)])
            add("pe", lambda h: h.matmul(psb[si][0:64, 0:8], lhsT=identb[:, 0:64], rhs=newbb[:, :], start=False, stop=True),
                reads=["identb", "newbb"], writes=[("ps", si)])
            add("act", lambda h: h.activation(out=Psn[0:64, 0:8], in_=psb[si][0:64, 0:8], func=AF.Exp), reads=[("ps", si)], writes=["Psn"])
            tpn = psb[TPP][:].bitcast(BF16)[0:8, 0:64]
            add("pe", lambda h: h.transpose(out=tpn, in_=Psn[0:64, 0:8], identity=identb[0:64, 0:64]),
                reads=["Psn", "identb"], writes=[("ps", TPP)])
            add("dve", lambda h: h.tensor_copy(out=PTn[0:8, :], in_=tpn), reads=[("ps", TPP)], writes=["PTn"])
            for hh in range(4):
                add("pe", lambda h, hh=hh, b=b: h.matmul(sacc_ap(hh), lhsT=PTn[0:8, hh * 16:(hh + 1) * 16], rhs=vaugn[0:8, b, hh, 0:129],
                                                          start=False, stop=True, skip_group_check=True),
                    reads=["PTn", ("Vn", b), ("Vn1",)], writes=[("ps", SACC[hh // 2])])
            for hh in range(4):
                a_ = sacc_ap(hh)
                rr = rl[0:16, 40 + hh: 41 + hh]
                add("dve", lambda h, a_=a_, rr=rr: h.reciprocal(out=rr, in_=a_[:, 128:129]), reads=[("ps", SACC[hh // 2])], writes=[("rls", hh)])
                add("dve", lambda h, rr=rr: h.tensor_tensor(out=rr, in0=rr, in1=small[0:16, 1:2], op=ALU.mult), reads=["sgn"], writes=[("rls", hh)])
                add("dve", lambda h, a_=a_, rr=rr, hh=hh: h.tensor_scalar(out=asb[0:16, hh * 128:(hh + 1) * 128], in0=a_[:, 0:128], scalar1=rr, scalar2=None, op0=ALU.mult),
                    reads=[("ps", SACC[hh // 2]), ("rls", hh)], writes=[("asb", hh)])
            add("dve", lambda h: h.tensor_copy(out=asbb[0:16, :], in_=asb[0:16, :]), reads=[("asb", h_) for h_ in range(4)], writes=["asbb"])
            add("pe", lambda h: h.matmul(psb[OPS][0:8, :], lhsT=selb[0:32, :], rhs=asbb[0:32, :], start=True, stop=True),
                reads=["selb", "asbb"], writes=[("ps", OPS)])
            add("dve", lambda h: h.tensor_copy(out=osb[0:8, :], in_=psb[OPS][0:8, :]), reads=[("ps", OPS)], writes=["osb"])
            for hh in range(4):
                add("act", lambda h, hh=hh: h.activation(out=asb[0:8, hh * 128:(hh + 1) * 128], in_=osb[0:8, hh * 128:(hh + 1) * 128], func=AF.Square,
                                                         accum_out=rl[0:8, 48 + hh:49 + hh]),
                    reads=["osb"], writes=[("asb", hh), "rls2"])
            rstd_ops(rl[0:8, 48:52], 128, "rls2")
            for hh in range(4):
                add("dve", lambda h, hh=hh: h.scalar_tensor_tensor(out=osbb[0:8, hh * 128:(hh + 1) * 128], in0=osb[0:8, hh * 128:(hh + 1) * 128],
                                                                   scalar=rl[0:8, 48 + hh:49 + hh], in1=sgbc[0:8, :], op0=ALU.mult, op1=ALU.mult),
                    reads=["osb", "rls2", "sgbc"], writes=["osbb"])
            tpo = psb[TPP][:].bitcast(BF16)[:, 0:32].rearrange("p (k n) -> p k n", k=4)
            for hh in range(4):
                add("pe", lambda h, hh=hh: h.transpose(out=tpo[:, hh, :], in_=osbb[0:8, hh * 128:(hh + 1) * 128], identity=identb[0:8, 0:8]),
                    reads=["osbb", "identb"], writes=[("ps", TPP)])
            add("act", lambda h, b=b: h.copy(out=mixTs[:, 4:8, b * 8:(b + 1) * 8], in_=tpo), reads=[("ps", TPP)], writes=[("mixas", b)])

        def do_D(i):
            if i % 16 == 0:
                batch_init(i // 16)
            stD(i)
            if i % 16 == 15:
                batch_fin(i // 16)

        if NG:
            stG(0)
            stG(1)
            stA(0)
            for i in range(NG):
                if i + 2 < NG:
                    stG(i + 2)
                if i + 1 < NG:
                    stA(i + 1)
                stB(i)
                if i >= 1:
                    do_D(i - 1)
                stC(i)
            do_D(NG - 1)

        if stage < 4:
            P.emit(nc, es)
            return nc
        p2_keys = (["zer", ("tbuf", 0), ("tbuf", 1), "osb", "osbb", "asbb", "selb", "rls2"] + [("accs", i) for i in range(3)]
                   + [("obuf", m_, h_) for m_ in range(4) for h_ in range(4)] + [("obf", o_, h_) for o_ in range(2) for h_ in range(4)]
                   + [("rl", h_, m_, i_) for h_ in range(4) for m_ in range(4) for i_ in range(2)]
                   + [("ssa", m_, h_) for m_ in range(4) for h_ in range(4)] + [("rls", h_) for h_ in range(4)] + [("asb", h_) for h_ in range(4)]
                   + [("PT", i, c_, b_) for i in range(3) for c_ in range(2) for b_ in range(4)] + [("kvp", i) for i in range(16)] + [("kvp1", i) for i in range(16)] + ["Psn", "PTn"]
                   + [("KTs", i, jj_) for i in range(2) for jj_ in range(4)] + [("Psb", i) for i in range(2)] + [("PTs", i) for i in range(2)]
                   + [("KT", j_, h_) for j_ in range(4) for h_ in range(4)] + [("QT", j_, h_) for j_ in range(4) for h_ in range(4)]
                   + [("V", t, h_) for t in range(16) for h_ in range(4)] + [("V1",)] + [("cu", c_) for c_ in range(4)] + [("cuh", c_) for c_ in range(4)]
                   + ["csb", "ct1", "ct2", ("kvst", 0), ("kvst", 1), ("xin", 0), ("xin", 1), ("hb", 0), ("hb", 1)]
                   + [("ss", 0), ("ss", 1)])
        B3 = Alloc(PH)
        x1 = B3([128, 16, 1024], F32)
        h2T = B3([128, 8, S], BF16)
        Wo = B3([128, 8, 1024], BF16)
        xin3 = [B3([128, 1024], F32) for _ in range(2)]
        hb3 = [B3([128, 1024], BF16) for _ in range(2)]
        ss3 = B3([128, 32], F32)
        A2_END = B3.o
        assert A2_END <= ARENA_B, A2_END

        add("pool", lambda h: h.dma_start(out=Wo[:], in_=w_out.rearrange("(k p) n -> p k n", p=128)),
            writes=["Wo"] + p2_keys + [("x1", t) for t in range(16)] + [("h2T", t) for t in range(16)]
            + [("xin3", 0), ("xin3", 1), ("hb3", 0), ("hb3", 1), ("ss3", 0), ("ss3", 1)], dma_key="Wo")
        add("sp", lambda h: h.dma_start(out=gbc[:], in_=n2g.partition_broadcast(128)), writes=["gbc"], dma_key="c5")

        def outproj_tile(xsrc, npart, lhs_ap, lkeys, x1ap, x1key, slot, dstT, dkeys, use_act):
            add("sp", lambda h: h.dma_start(out=xin3[slot][:npart], in_=xsrc), writes=[("xin3", slot)], dma_key=("xin3", slot))
            pis = []
            for nh in range(2):
                pi = nextps()
                pis.append(pi)
                for fc in range(8):
                    add("pe", lambda h, fc=fc, pi=pi, nh=nh: h.matmul(psb[pi][0:npart, :], lhsT=lhs_ap(fc), rhs=Wo[:, fc, nh * 512:(nh + 1) * 512],
                                                                       start=(fc == 0), stop=(fc == 7)),
                        reads=["Wo"] + lkeys, writes=[("ps", pi)])
            for nh in range(2):
                add("dve", lambda h, nh=nh, pi=pis[nh]: h.tensor_tensor(out=x1ap[:, nh * 512:(nh + 1) * 512], in0=psb[pi][0:npart, :],
                                                                         in1=xin3[slot][:npart, nh * 512:(nh + 1) * 512], op=ALU.add),
                    reads=[("ps", pis[nh]), ("xin3", slot)], writes=[x1key])
            sa = ss3[:npart, slot:slot + 1]
            add("act", lambda h: h.activation(out=hb3[slot][:npart], in_=x1ap, func=AF.Square, accum_out=sa),
                reads=[x1key], writes=[("hb3", slot), ("ss3", slot)])
            rstd_ops(sa, D, ("ss3", slot))
            add("dve", lambda h: h.scalar_tensor_tensor(out=hb3[slot][:npart], in0=x1ap, scalar=sa, in1=gbc[:npart], op0=ALU.mult, op1=ALU.mult),
                reads=[x1key, ("ss3", slot), "gbc"], writes=[("hb3", slot)])

            def part2():
                pi = nextps()
                tp = psb[pi][:].bitcast(BF16)[:, 0:8 * 128].rearrange("p (k n) -> p k n", k=8)
                for kc in range(8):
                    add("pe", lambda h, kc=kc: h.transpose(out=tp[:, kc, 0:npart], in_=hb3[slot][:npart, kc * 128:(kc + 1) * 128],
                                                           identity=identb[:npart, :npart]),
                        reads=[("hb3", slot), "identb"], writes=[("ps", pi)])
                if use_act:
                    add("act", lambda h: h.copy(out=dstT, in_=tp[:, :, 0:npart]), reads=[("ps", pi)], writes=dkeys)
                else:
                    add("dve", lambda h: h.tensor_copy(out=dstT, in_=tp[:, :, 0:npart]), reads=[("ps", pi)], writes=dkeys)
            return part2

        pend = None
        for t in range(16):
            lk = [("mixc", cc, t // 4) for cc in range(4)] + [("mixa", t)]
            p2_ = outproj_tile(xp[t * 128:(t + 1) * 128, :], 128, lambda fc, t=t: mixT[:, fc, t * 128:(t + 1) * 128], lk,
                               x1[:, t, :], ("x1", t), t % 2, h2T[:, :, t * 128:(t + 1) * 128], [("h2T", t)], t % 2 == 0)
            if pend is not None:
                pend()
            pend = p2_
        p2_ = outproj_tile(xs, 32, lambda fc: mixTs[:, fc, :], [("mixcs", cc) for cc in range(4)] + [("mixas", b) for b in range(4)],
                           x1s[0:32, :], "x1s", 0, h2Ts[:], ["h2Ts"], True)
        pend()
        p2_()

        if stage < 5:
            P.emit(nc, es)
            return nc
        a2_tail = ["Wo", ("xin3", 0), ("xin3", 1), ("hb3", 0), ("hb3", 1), ("ss3", 0), ("ss3", 1)]
        mix_keys = [("mixc", cc, j_) for cc in range(4) for j_ in range(4)] + [("mixa", t) for t in range(16)]
        BM = Alloc(MIX_OFF)
        Wu = [BM([128, 8, 512], BF16) for _ in range(2)]
        Wd = [BM([128, 4, 1024], BF16) for _ in range(2)]
        assert BM.o <= PH
        B4 = Alloc(PH + 96 * KB)
        aT = [B4([128, 4, 512], BF16) for _ in range(2)]
        aTs = B4([128, 4, 32], BF16)
        sq = [B4([128, 512], F32) for _ in range(2)]
        yst = [B4([128, 1024], F32) for _ in range(2)]
        ss4 = B4([128, 32], F32)
        assert B4.o <= ARENA_B, B4.o

        add("sp", lambda h: h.dma_start(out=gbc[:], in_=fing.partition_broadcast(128)), writes=["gbc"], dma_key="c5")

        def final_tile(x1ap, x1key, npart, slot, dst):
            sa = ss4[:npart, slot:slot + 1]
            add("act", lambda h: h.activation(out=yst[slot][:npart], in_=x1ap, func=AF.Square, accum_out=sa),
                reads=[x1key], writes=[("yst", slot), ("ss4", slot)])
            rstd_ops(sa, D, ("ss4", slot))
            add("dve", lambda h: h.scalar_tensor_tensor(out=yst[slot][:npart], in0=x1ap, scalar=sa, in1=gbc[:npart], op0=ALU.mult, op1=ALU.mult),
                reads=[x1key, ("ss4", slot), "gbc"], writes=[("yst", slot)])
            add("sp", lambda h: h.dma_start(out=dst, in_=yst[slot][:npart]), reads=[("yst", slot)], dma_key=("yst", slot))


        sqc = [0]
        for fb in range(8):
            ws = fb % 2
            extra = (mix_keys + a2_tail + [("Wu", 1), ("Wd", 0), ("Wd", 1)] + [("aT", a_, f_) for a_ in range(2) for f_ in range(4)] + ["aTs", ("sq", 0), ("sq", 1),
                                           ("yst", 0), ("yst", 1), ("ss4", 0), ("ss4", 1)]) if fb == 0 else []
            add("pool", lambda h, ws=ws, fb=fb: h.dma_start(out=Wu[ws][:], in_=w_up[:, fb * 512:(fb + 1) * 512].rearrange("(k p) n -> p k n", p=128)),
                writes=[("Wu", ws)] + extra, dma_key=("Wu", ws))
            add("pool", lambda h, ws=ws, fb=fb: h.dma_start(out=Wd[ws][:], in_=w_down[fb * 512:(fb + 1) * 512, :].rearrange("(k p) n -> p k n", p=128)),
                writes=[("Wd", ws)], dma_key=("Wd", ws))
            for j in range(4):
                as_ = (fb * 4 + j) % 2
                for fc in range(4):
                    pi = nextps()
                    for kc in range(8):
                        add("pe", lambda h, kc=kc, pi=pi, fc=fc, ws=ws, j=j: h.matmul(psb[pi][:], lhsT=Wu[ws][:, kc, fc * 128:(fc + 1) * 128],
                                                                                      rhs=h2T[:, kc, j * 512:(j + 1) * 512], start=(kc == 0), stop=(kc == 7)),
                            reads=[("Wu", ws)] + [("h2T", 4 * j + i) for i in range(4)], writes=[("ps", pi)])
                    sqs = sqc[0] % 2
                    sqc[0] += 1
                    add("act", lambda h, pi=pi, sqs=sqs: h.activation(out=sq[sqs][:], in_=psb[pi][:], func=AF.Square),
                        reads=[("ps", pi)], writes=[("sq", sqs)])
                    add("dve", lambda h, pi=pi, sqs=sqs, as_=as_, fc=fc: h.scalar_tensor_tensor(
                        out=aT[as_][:, fc, :], in0=psb[pi][:], scalar=0.0, in1=sq[sqs][:], op0=ALU.is_gt, op1=ALU.mult),
                        reads=[("ps", pi), ("sq", sqs)], writes=[("aT", as_, fc)])
                for tt in range(4):
                    t = 4 * j + tt
                    pis = []
                    for nh in range(2):
                        pi = nextps()
                        pis.append(pi)
                        for fc in range(4):
                            add("pe", lambda h, fc=fc, pi=pi, nh=nh, as_=as_, tt=tt, ws=ws: h.matmul(
                                psb[pi][:], lhsT=aT[as_][:, fc, tt * 128:(tt + 1) * 128], rhs=Wd[ws][:, fc, nh * 512:(nh + 1) * 512],
                                start=(fc == 0), stop=(fc == 3)),
                                reads=[("aT", as_, fc), ("Wd", ws)], writes=[("ps", pi)])
                    for nh in range(2):
                        add("dve", lambda h, nh=nh, pi=pis[nh], t=t: h.tensor_tensor(out=x1[:, t, nh * 512:(nh + 1) * 512], in0=psb[pi][:],
                                                                                      in1=x1[:, t, nh * 512:(nh + 1) * 512], op=ALU.add),
                            reads=[("ps", pis[nh])], writes=[("x1", t)])
                    if fb == 7:
                        final_tile(x1[:, t, :], ("x1", t), 128, t % 2, yp[t * 128:(t + 1) * 128, :])
            for fc in range(4):
                pi = nextps()
                for kc in range(8):
                    add("pe", lambda h, kc=kc, pi=pi, fc=fc, ws=ws: h.matmul(psb[pi][:, 0:32], lhsT=Wu[ws][:, kc, fc * 128:(fc + 1) * 128],
                                                                             rhs=h2Ts[:, kc, :], start=(kc == 0), stop=(kc == 7)),
                        reads=[("Wu", ws), "h2Ts"], writes=[("ps", pi)])
                sqs = sqc[0] % 2
                sqc[0] += 1
                add("act", lambda h, pi=pi, sqs=sqs: h.activation(out=sq[sqs][:, 0:32], in_=psb[pi][:, 0:32], func=AF.Square),
                    reads=[("ps", pi)], writes=[("sq", sqs)])
                add("dve", lambda h, pi=pi, sqs=sqs, fc=fc: h.scalar_tensor_tensor(
                    out=aTs[:, fc, :], in0=psb[pi][:, 0:32], scalar=0.0, in1=sq[sqs][:, 0:32], op0=ALU.is_gt, op1=ALU.mult),
                    reads=[("ps", pi), ("sq", sqs)], writes=["aTs"])
            pis = []
            for nh in range(2):
                pi = nextps()
                pis.append(pi)
                for fc in range(4):
                    add("pe", lambda h, fc=fc, pi=pi, nh=nh, ws=ws: h.matmul(psb[pi][0:32, :], lhsT=aTs[:, fc, :], rhs=Wd[ws][:, fc, nh * 512:(nh + 1) * 512],
                                                                             start=(fc == 0), stop=(fc == 3)),
                        reads=["aTs", ("Wd", ws)], writes=[("ps", pi)])
            for nh in range(2):
                add("dve", lambda h, nh=nh, pi=pis[nh]: h.tensor_tensor(out=x1s[0:32, nh * 512:(nh + 1) * 512], in0=psb[pi][0:32, :],
                                                                         in1=x1s[0:32, nh * 512:(nh + 1) * 512], op=ALU.add),
                    reads=[("ps", pis[nh])], writes=["x1s"])

        final_tile(x1s[0:32, :], "x1s", 32, 0, ys)

        P.emit(nc, es)
    return nc


_NC = None


def kernel(x_prompt, x_sample, cache_k, cache_v, state_conv, page_table, norm1_g, w_in, conv_w,
           lambda_q1, lambda_k1, lambda_q2, lambda_k2, subln_g, w_out, norm2_g, w_up, w_down, final_g):
    global _NC
    if _NC is None:
        _NC = build()
    nc = _NC
    f = lambda a: np.ascontiguousarray(np.asarray(a), dtype=np.float32)
    ckv = np.concatenate([f(cache_k).reshape(NPOOL * 128, 512), f(cache_v).reshape(NPOOL * 128, 512)], axis=1)
    lamv = np.concatenate([f(lambda_q1)[0], f(lambda_k1)[0], f(lambda_q2)[0], f(lambda_k2)[0]])[None, :]
    ctab = make_ctab()
    shared = dict(ckv=ckv, n1g=f(norm1_g), w_in=f(w_in)[0], conv_w=f(conv_w)[0], lamv=lamv, subg=f(subln_g),
                  w_out=f(w_out)[0], n2g=f(norm2_g), w_up=f(w_up)[0], w_down=f(w_down)[0], fing=f(final_g)[None, :], ctab=ctab)
    xp = f(x_prompt)
    xs = f(x_sample)
    sc = f(state_conv)[0]
    pt = np.ascontiguousarray(np.asarray(page_table), dtype=np.int32)
    in_maps = []
    for c in range(NCORES):
        m = dict(shared)
        m["xp"] = xp[c]
        m["xs"] = xs[4 * c:4 * c + 4].reshape(32, D)
        m["sconv"] = sc[4 * c:4 * c + 4]
        m["ptab"] = pt[4 * c:4 * c + 4].reshape(1, 256)
        in_maps.append(m)
    res = run_bass_kernel_spmd(nc, in_maps, core_ids=list(range(NCORES)))
    r = res.results
    y_prompt = np.stack([r[c]["yp"] for c in range(NCORES)])
    y_sample = np.concatenate([r[c]["ys"].reshape(4, 8, D) for c in range(NCORES)])
    k_prompt = np.stack([r[c]["kp"].reshape(S, 4, 2, 64) for c in range(NCORES)])[None]
    v_prompt = np.stack([r[c]["vp"].reshape(S, 4, 128) for c in range(NCORES)])[None]
    conv_prompt = np.stack([r[c]["cp"] for c in range(NCORES)])[None]
    k_sample = np.concatenate([r[c]["ks"].reshape(4, 8, 4, 2, 64) for c in range(NCORES)])[None]
    v_sample = np.concatenate([r[c]["vs"].reshape(4, 8, 4, 128) for c in range(NCORES)])[None]
    conv_sample = np.concatenate([r[c]["cs"] for c in range(NCORES)])[None]
    return (y_prompt.astype(np.float32), y_sample.astype(np.float32), k_prompt.astype(np.float32), v_prompt.astype(np.float32),
            conv_prompt.astype(np.float32), k_sample.astype(np.float32), v_sample.astype(np.float32), conv_sample.astype(np.float32))
```

```python
import numpy as np
from contextlib import ExitStack
import concourse.bass as bass
import concourse.mybir as mybir
from concourse.bass_utils import run_bass_kernel_spmd

F32 = mybir.dt.float32
BF16 = mybir.dt.bfloat16
I32 = mybir.dt.int32
AF = mybir.ActivationFunctionType
ALU = mybir.AluOpType

ENGS = ("pe", "act", "dve", "pool", "sp")
NCORES = 8
D = 1024
S = 2048
NT = 16
DIN = 3072
DFF = 4096
NPAGE = 64
PAST = 8192
NPOOL = 2560
EPS = 1e-5
SLOPES = [2.0 ** (-8.0 * (h + 1) / 4) for h in range(4)]
LAM_INIT = 0.2
NEGBIG = -30000.0
import os as _os
CUT = int(_os.environ.get('KCUT', '100000000'))


class Op:
    __slots__ = ("eng", "fn", "idx", "waits", "flag", "dma_key", "dma_val", "vc")

    def __init__(self, eng, fn, idx):
        self.eng = eng
        self.fn = fn
        self.idx = idx
        self.waits = []
        self.flag = False
        self.dma_key = None
        self.dma_val = 0
        self.vc = None


class Prog:
    def __init__(self):
        self.ops = {e: [] for e in ENGS}
        self.vc = {e: {} for e in ENGS}
        self.last_w = {}
        self.readers = {}
        self.dma_cnt = {}

    def _dep(self, op, p):
        e = op.eng
        vc = self.vc[e]
        if p.dma_key is not None:
            k = ("dma", p.dma_key)
            if vc.get(k, 0) >= p.dma_val:
                return
            tot = self.dma_cnt[p.dma_key]
            op.waits.append(("dma", p.dma_key, tot))
            vc[k] = tot
        else:
            if p.eng == e and e == "pe":
                return
            if vc.get(p.eng, -1) >= p.idx:
                return
            op.waits.append(("eng", p.eng, p))
            p.flag = True
        for k, v in p.vc.items():
            if vc.get(k, -1) < v:
                vc[k] = v

    def add(self, eng, fn, reads=(), writes=(), dma_key=None):
        self.nadd = getattr(self, "nadd", 0) + 1
        if self.nadd > CUT:
            return None
        ops = self.ops[eng]
        op = Op(eng, fn, len(ops))
        deps = []
        for b in reads:
            w = self.last_w.get(b)
            if w is not None:
                deps.append(w)
        for b in writes:
            w = self.last_w.get(b)
            if w is not None:
                deps.append(w)
            deps.extend(self.readers.get(b, ()))
        seen = set()
        for p in deps:
            if id(p) in seen:
                continue
            seen.add(id(p))
            self._dep(op, p)
        if dma_key is not None:
            op.dma_key = dma_key
            self.dma_cnt[dma_key] = self.dma_cnt.get(dma_key, 0) + 16
            op.dma_val = self.dma_cnt[dma_key]
        vc = self.vc[eng]
        if eng == "pe":
            vc[eng] = op.idx
        snap = dict(vc)
        snap[eng] = op.idx
        if dma_key is not None:
            snap[("dma", dma_key)] = op.dma_val
        op.vc = snap
        for b in reads:
            self.readers.setdefault(b, []).append(op)
        for b in writes:
            self.last_w[b] = op
            self.readers[b] = []
        ops.append(op)
        return op

    def emit(self, nc, es, final_eng="sp"):
        sem_e = {e: es.enter_context(nc.semaphore("c_" + e)) for e in ENGS}
        sem_d = {}
        for i, k in enumerate(self.dma_cnt):
            sem_d[k] = es.enter_context(nc.semaphore("d%d" % i))
        rank = {}
        for e in ENGS:
            r = 0
            for op in self.ops[e]:
                if op.flag:
                    r += 1
                    rank[id(op)] = r
        finals = list(self.dma_cnt.items())

        def run(e, h):
            for op in self.ops[e]:
                for w in op.waits:
                    if w[0] == "dma":
                        h.wait_ge(sem_d[w[1]], w[2])
                    else:
                        h.wait_ge(sem_e[w[1]], rank[id(w[2])])
                ins = op.fn(h)
                if op.dma_key is not None:
                    ins.then_inc(sem_d[op.dma_key], 16)
                elif op.flag:
                    ins.then_inc(sem_e[e], 1)
            if e == final_eng:
                for k, v in finals:
                    h.wait_ge(sem_d[k], v)

        with nc.Block() as block:
            @block.tensor
            def _(h):
                run("pe", h)

            @block.scalar
            def _(h):
                run("act", h)

            @block.vector
            def _(h):
                run("dve", h)

            @block.gpsimd
            def _(h):
                run("pool", h)

            @block.sync
            def _(h):
                run("sp", h)


C_TABP = 0
C_MASK = 64
C_IDENT = 192
C_SEL = 320
C_M1 = 328
C_M2 = 329
C_NEWB = 330
C_AUXR = 338
C_AUXL = 850
C_TABP2 = 850 + 1024
C_N = 850 + 1024 + 76


def make_ctab():
    t = np.zeros((128, C_N), np.float32)
    p = np.arange(128, dtype=np.float32)
    for h in range(4):
        for d in range(16):
            t[:, C_TABP + h * 16 + d] = SLOPES[h] * (p - 128.0 * d)
    for h in range(4):
        for d in range(-3, 16):
            t[:, C_TABP2 + h * 19 + d + 3] = SLOPES[h] * (p - 128.0 * d)
    k = np.arange(128)[:, None]
    q = np.arange(128)[None, :]
    t[:, C_MASK:C_MASK + 128] = (q >= k).astype(np.float32)
    t[:, C_IDENT:C_IDENT + 128] = np.eye(128, dtype=np.float32)
    t[0:8, C_SEL:C_SEL + 8] = np.eye(8)
    t[8:16, C_SEL:C_SEL + 8] = np.eye(8)
    t[0:8, C_M1] = 1.0
    t[8:16, C_M2] = 1.0
    for h in range(4):
        for c in range(2):
            for qq in range(8):
                r = h * 16 + c * 8 + qq
                for kk in range(8):
                    t[r, C_NEWB + kk] = SLOPES[h] * kk + (NEGBIG if kk > qq else 0.0)
    col = np.arange(512)
    t[0, C_AUXR:C_AUXR + 512] = 1.0
    t[1, C_AUXR:C_AUXR + 512] = 128.0 * (col // 128)
    t[2, C_AUXR:C_AUXR + 512] = (col % 128)
    for g in range(16):
        for h in range(4):
            base = C_AUXL + g * 64 + h * 16
            t[0, base:base + 16] = SLOPES[h] * 512.0 * (g - 16)
            t[1, base:base + 16] = SLOPES[h]
            t[2, base:base + 16] = SLOPES[h]
    return t


def build(stage=9, kv_rows=NPOOL * 128, skip=()):
    nc = bass.Bass("TRN2", target_bir_lowering=False)

    def din(name, shape, dt=F32):
        return nc.dram_tensor(name, list(shape), dt, kind="ExternalInput").ap()

    def dout(name, shape):
        return nc.dram_tensor(name, list(shape), F32, kind="ExternalOutput").ap()

    xp = din("xp", [S, D])
    xs = din("xs", [32, D])
    ckv = din("ckv", [kv_rows, 1024])
    sconv = din("sconv", [4, 2, 512])
    ptab = din("ptab", [1, 256], I32)
    n1g = din("n1g", [1, D])
    w_in = din("w_in", [D, DIN])
    conv_w = din("conv_w", [3, 512])
    lamv = din("lamv", [1, 256])
    subg = din("subg", [1, 128])
    w_out = din("w_out", [D, D])
    n2g = din("n2g", [1, D])
    w_up = din("w_up", [D, DFF])
    w_down = din("w_down", [DFF, D])
    fing = din("fing", [1, D])
    ctab = din("ctab", [128, C_N])

    yp = dout("yp", [S, D])
    ys = dout("ys", [32, D])
    kp = dout("kp", [S, 512])
    vp = dout("vp", [S, 512])
    cp = dout("cp", [2, 512])
    ks = dout("ks", [32, 512])
    vs = dout("vs", [32, 512])
    cs = dout("cs", [4, 2, 512])

    es = ExitStack()
    with es:
        ARENA_B = 206 * 1024
        arena = es.enter_context(nc.sbuf_tensor("arena", [128, ARENA_B // 2], BF16))
        psall = es.enter_context(nc.psum_tensor("psall", [128, 4096], F32))
        psb = [psall[:, i * 512:(i + 1) * 512] for i in range(8)]

        def R(off, shape, dt):
            n = 1
            for s_ in shape[1:]:
                n *= s_
            esz = 4 if dt in (F32, I32) else 2
            assert off % 4 == 0
            assert off + n * esz <= ARENA_B, (off, shape)
            v = arena[:, off // 2: off // 2 + n * esz // 2]
            if esz == 4:
                v = v.bitcast(dt)
            if len(shape) == 3:
                v = v.rearrange("p (a b) -> p a b", a=shape[1])
            elif len(shape) == 4:
                v = v.rearrange("p (a b c) -> p a b c", a=shape[1], b=shape[2])
            return v

        class Alloc:
            def __init__(self, base):
                self.o = base

            def __call__(self, shape, dt):
                n = 1
                for s_ in shape[1:]:
                    n *= s_
                nb = n * (4 if dt in (F32, I32) else 2)
                nb = (nb + 31) // 32 * 32
                v = R(self.o, shape, dt)
                self.o += nb
                return v

        KB = 1024
        A = Alloc(0)
        ctf = A([128, C_N], F32)
        identb = A([128, 128], BF16)
        maskb = A([128, 128], BF16)
        auxb = A([128, 512 + 1024], BF16)
        newbb = A([128, 8], BF16)
        gbc = A([128, 1024], F32)
        sgbc = A([128, 128], F32)
        convw = A([128, 4, 3], F32)
        lamb = A([128, 256], F32)
        small = A([128, 64], F32)
        ptb = A([128, 256], I32)
        idx = A([128, 256], I32)
        iot = A([128, 2], F32)
        x1s = A([128, 1024], F32)
        hTs = A([128, 8, 32], BF16)
        mixTs = A([128, 8, 32], BF16)
        h2Ts = A([128, 8, 32], BF16)
        qbd = A([128, 4, 4, 64], BF16)
        kTn = A([128, 4, 32], BF16)
        vaugn = A([128, 4, 4, 130], BF16)
        cus = A([128, 4, 4, 10], F32)
        PERSIST_END = A.o
        assert PERSIST_END <= 40 * KB, PERSIST_END
        MIX_OFF = 40 * KB
        mixT = R(MIX_OFF, [128, 8, S], BF16)
        PH = 72 * KB

        NEGLAM = small[:, 0:1]
        SGN = small[:, 1:2]

        P = Prog()
        add = P.add

        add("sp", lambda h: h.dma_start(out=ctf[:], in_=ctab), writes=["ctf"], dma_key="c0")
        add("sp", lambda h: h.dma_start(out=lamb[:], in_=lamv.partition_broadcast(128)), writes=["lamb"], dma_key="c1")
        add("sp", lambda h: h.dma_start(out=sgbc[:], in_=subg.partition_broadcast(128)), writes=["sgbc"], dma_key="c2")
        for cc in range(4):
            add("sp", lambda h, cc=cc: h.dma_start(out=convw[:, cc, :], in_=conv_w[:, cc * 128:(cc + 1) * 128].rearrange("r p -> p r"),
                                                   allow_slow_non_contiguous=True), writes=[("convw", cc)], dma_key="c3")
        add("sp", lambda h: h.dma_start(out=ptb[:], in_=ptab.partition_broadcast(128)), writes=["ptb"], dma_key="c4")
        add("sp", lambda h: h.dma_start(out=gbc[:], in_=n1g.partition_broadcast(128)), writes=["gbc"], dma_key="c5")
        add("pool", lambda h: h.iota(idx[:, 0:1], [[0, 1]], base=0, channel_multiplier=1), writes=["iot0"])
        add("dve", lambda h: h.tensor_copy(out=iot[:, 0:1], in_=idx[:, 0:1]), reads=["iot0"], writes=["iot"])
        add("dve", lambda h: h.tensor_scalar(out=idx[:], in0=ptb[:], scalar1=128.0, scalar2=iot[:, 0:1],
                                             op0=ALU.mult, op1=ALU.add), reads=["ptb", "iot", "iot0"], writes=["idx"])
        add("dve", lambda h: h.tensor_copy(out=identb[:], in_=ctf[:, C_IDENT:C_IDENT + 128]), reads=["ctf"], writes=["identb"])
        add("dve", lambda h: h.tensor_copy(out=maskb[:], in_=ctf[:, C_MASK:C_MASK + 128]), reads=["ctf"], writes=["maskb"])
        add("dve", lambda h: h.tensor_copy(out=auxb[:], in_=ctf[:, C_AUXR:C_AUXR + 1536]), reads=["ctf"], writes=["auxb"])
        add("dve", lambda h: h.tensor_copy(out=newbb[:], in_=ctf[:, C_NEWB:C_NEWB + 8]), reads=["ctf"], writes=["newbb"])
        add("dve", lambda h: h.tensor_scalar(out=sgbc[:], in0=sgbc[:], scalar1=1.0 - LAM_INIT, scalar2=None, op0=ALU.mult),
            reads=["sgbc"], writes=["sgbc"])
        add("dve", lambda h: h.tensor_tensor(out=lamb[:, 0:64], in0=lamb[:, 0:64], in1=lamb[:, 64:128], op=ALU.mult),
            reads=["lamb"], writes=["lamb"])
        add("dve", lambda h: h.tensor_tensor(out=lamb[:, 128:192], in0=lamb[:, 128:192], in1=lamb[:, 192:256], op=ALU.mult),
            reads=["lamb"], writes=["lamb"])
        add("dve", lambda h: h.reduce_sum(out=small[:, 2:3], in_=lamb[:, 0:64], axis=mybir.AxisListType.X),
            reads=["lamb"], writes=["s2"])
        add("dve", lambda h: h.reduce_sum(out=small[:, 3:4], in_=lamb[:, 128:192], axis=mybir.AxisListType.X),
            reads=["lamb"], writes=["s3"])
        add("act", lambda h: h.activation(out=small[:, 2:4], in_=small[:, 2:4], func=AF.Exp), reads=["s2", "s3"], writes=["s2", "s3"])
        add("dve", lambda h: h.tensor_tensor(out=small[:, 0:1], in0=small[:, 3:4], in1=small[:, 2:3], op=ALU.subtract),
            reads=["s2", "s3"], writes=["neglam"])
        add("dve", lambda h: h.tensor_scalar(out=small[:, 0:1], in0=small[:, 0:1], scalar1=-LAM_INIT, scalar2=None, op0=ALU.add),
            reads=["neglam"], writes=["neglam"])
        add("dve", lambda h: h.scalar_tensor_tensor(out=small[0:16, 1:2], in0=ctf[0:16, C_M2:C_M2 + 1], scalar=small[0:16, 0:1],
                                                    in1=ctf[0:16, C_M1:C_M1 + 1], op0=ALU.mult, op1=ALU.add),
            reads=["neglam", "ctf"], writes=["sgn"])

        psrr = [0]

        def nextps():
            i = psrr[0] % 8
            psrr[0] += 1
            return i

        def rstd_ops(ssap, n, key, npart=128):
            add("dve", lambda h: h.tensor_scalar(out=ssap, in0=ssap, scalar1=1.0 / n, scalar2=EPS, op0=ALU.mult, op1=ALU.add),
                reads=[key], writes=[key])
            add("act", lambda h: h.activation(out=ssap, in_=ssap, func=AF.Ln), reads=[key], writes=[key])
            add("act", lambda h: h.activation(out=ssap, in_=ssap, func=AF.Exp, scale=-0.5), reads=[key], writes=[key])

        B1 = Alloc(PH)
        hT = B1([128, 8, 1024], BF16)
        Wsl = [B1([128, 8, 512], BF16) for _ in range(4)]
        KT = B1([128, 4, S], BF16)
        QT = B1([128, 4, S], BF16)
        Vaug = B1([128, 16, 4, 130], BF16)
        CU_OFF = B1.o
        cu = B1([128, 4, 514], F32)
        csb = B1([128, 512], F32)
        ct1 = B1([128, 512], F32)
        ct2 = B1([128, 512], F32)
        _pad = [B1([128, 512], F32) for _ in range(2)]
        kvst = [R(MIX_OFF + 16 * KB + i * 2048, [128, 512], F32) for i in range(2)]
        xin = [B1([128, 1024], F32) for _ in range(2)]
        hb = [B1([128, 1024], BF16) for _ in range(2)]
        ssq = B1([128, 32], F32)
        P1_END = B1.o
        assert P1_END <= ARENA_B, P1_END

        add("pool", lambda h: h.memset(Vaug[:, :, :, 128:129], 1.0), writes=[("V1",)])
        add("pool", lambda h: h.memset(vaugn[:, :, :, 128:129], 1.0), writes=[("Vn1",)])
        add("pool", lambda h: h.memset(cu[:, :, 0:2], 0.0), writes=[("cuh", c_) for c_ in range(4)])
        add("pool", lambda h: h.memset(qbd[:], 0.0), writes=["qbd"])

        wcount = [0]

        def load_w(src_ap, ncols_ok=True):
            sl = wcount[0] % 4
            wcount[0] += 1
            add("pool", lambda h: h.dma_start(out=Wsl[sl][:], in_=src_ap), writes=[("W", sl)], dma_key=("W", sl))
            return sl

        def win_block(col0):
            return w_in[:, col0:col0 + 512].rearrange("(k p) n -> p k n", p=128)

        def norm_tile(src_ap, slot, npart, gkey="gbc"):
            add("sp", lambda h: h.dma_start(out=xin[slot][:npart], in_=src_ap), writes=[("xin", slot)], dma_key=("xin", slot))
            sa = ssq[:npart, slot:slot + 1]
            add("act", lambda h: h.activation(out=hb[slot][:npart], in_=xin[slot][:npart], func=AF.Square, accum_out=sa),
                reads=[("xin", slot)], writes=[("hb", slot), ("ss", slot)])
            rstd_ops(sa, D, ("ss", slot))
            add("dve", lambda h: h.scalar_tensor_tensor(out=hb[slot][:npart], in0=xin[slot][:npart], scalar=sa,
                                                        in1=gbc[:npart], op0=ALU.mult, op1=ALU.mult),
                reads=[("xin", slot), ("ss", slot), gkey], writes=[("hb", slot)])

        def transpose_to(slot, npart, dst_ap, dkeys, use_act):
            pi = nextps()
            tp = psb[pi][:].bitcast(BF16)[:, 0:8 * 128].rearrange("p (k n) -> p k n", k=8)
            for kc in range(8):
                add("pe", lambda h, kc=kc: h.transpose(out=tp[:, kc, 0:npart], in_=hb[slot][:npart, kc * 128:(kc + 1) * 128],
                                                       identity=identb[:npart, :npart]),
                    reads=[("hb", slot), "identb"], writes=[("ps", pi)])
            if use_act:
                add("act", lambda h: h.copy(out=dst_ap, in_=tp[:, :, 0:npart]), reads=[("ps", pi)], writes=dkeys)
            else:
                add("dve", lambda h: h.tensor_copy(out=dst_ap, in_=tp[:, :, 0:npart]), reads=[("ps", pi)], writes=dkeys)

        norm_tile(xs, 0, 32)
        transpose_to(0, 32, hTs[:], ["hTs"], True)

        def proj_fm(wsl, hh, rhs_ap, n, rkeys):
            pi = nextps()
            for kc in range(8):
                add("pe", lambda h, kc=kc: h.matmul(psb[pi][:, 0:n], lhsT=Wsl[wsl][:, kc, hh * 128:(hh + 1) * 128],
                                                    rhs=rhs_ap(kc), start=(kc == 0), stop=(kc == 7)),
                    reads=[("W", wsl)] + rkeys, writes=[("ps", pi)])
            return pi

        def proj_tm(wsl, lhs_ap, m, rkeys):
            pi = nextps()
            for kc in range(8):
                add("pe", lambda h, kc=kc: h.matmul(psb[pi][0:m, :], lhsT=lhs_ap(kc), rhs=Wsl[wsl][:, kc, :],
                                                    start=(kc == 0), stop=(kc == 7)),
                    reads=[("W", wsl)] + rkeys, writes=[("ps", pi)])
            return pi

        stc = [0]

        for hs in range(2):
            def p0_tile(hs_, t):
                gt = hs_ * 8 + t
                slot = gt % 2
                norm_tile(xp[gt * 128:(gt + 1) * 128, :], slot, 128)
                transpose_to(slot, 128, hT[:, :, t * 128:(t + 1) * 128], [("hT", t)], t % 2 == 0)

            for t in range(4 if hs == 1 else 0, 8):
                p0_tile(hs, t)
            hkeys = [("hT", t) for t in range(8)]

            wk = load_w(win_block(1536 + 512))
            for cj in range(2):
                gj = hs * 2 + cj
                for hh in range(4):
                    pi = proj_fm(wk, hh, lambda kc, cj=cj: hT[:, kc, cj * 512:(cj + 1) * 512], 512, hkeys[cj * 4:cj * 4 + 4])
                    add("act", lambda h, pi=pi, hh=hh, gj=gj: h.copy(out=KT[:, hh, gj * 512:(gj + 1) * 512], in_=psb[pi][:]),
                        reads=[("ps", pi)], writes=[("KT", gj, hh)])
            for t in range(8):
                gt = hs * 8 + t
                pi = proj_tm(wk, lambda kc, t=t: hT[:, kc, t * 128:(t + 1) * 128], 128, [("hT", t)])
                st = stc[0] % 2
                stc[0] += 1
                if _os.environ.get("KVACT") == "2":
                    add("act", lambda h, pi=pi, st=st: h.copy(out=csb[:], in_=psb[pi][:]), reads=[("ps", pi)], writes=[("kvst", st)])
                elif _os.environ.get("KVACT") == "3":
                    add("act", lambda h, pi=pi, st=st: h.copy(out=kvst[st][:], in_=hT[:, 0, 0:512]), reads=[("ps", pi)], writes=[("kvst", st)])
                elif _os.environ.get("KVACT"):
                    add("act", lambda h, pi=pi, st=st: h.copy(out=kvst[st][:], in_=psb[pi][:]), reads=[("ps", pi)], writes=[("kvst", st)])
                else:
                    add("dve", lambda h, pi=pi, st=st: h.tensor_copy(out=kvst[st][:], in_=psb[pi][:]), reads=[("ps", pi)], writes=[("kvst", st)])
                add("sp", lambda h, st=st, gt=gt: h.dma_start(out=kp[gt * 128:(gt + 1) * 128, :], in_=kvst[st][:]),
                    reads=[("kvst", st)], dma_key=("kvst", st))
            if hs == 1:
                for hh in range(4):
                    pi = proj_fm(wk, hh, lambda kc: hTs[:, kc, :], 32, ["hTs"])
                    add("act", lambda h, pi=pi, hh=hh: h.copy(out=kTn[:, hh, :], in_=psb[pi][:, 0:32]), reads=[("ps", pi)], writes=["kTn"])
                pi = proj_tm(wk, lambda kc: hTs[:, kc, :], 32, ["hTs"])
                st = stc[0] % 2
                stc[0] += 1
                add("dve", lambda h, pi=pi, st=st: h.tensor_copy(out=kvst[st][0:32], in_=psb[pi][0:32, :]), reads=[("ps", pi)], writes=[("kvst", st)])
                add("sp", lambda h, st=st: h.dma_start(out=ks, in_=kvst[st][0:32]), reads=[("kvst", st)], dma_key=("kvst", st))

            wv = load_w(win_block(1536 + 1024))
            for t in range(8):
                gt = hs * 8 + t
                pi = proj_tm(wv, lambda kc, t=t: hT[:, kc, t * 128:(t + 1) * 128], 128, [("hT", t)])
                st = stc[0] % 2
                stc[0] += 1
                add("dve", lambda h, pi=pi, st=st: h.tensor_copy(out=kvst[st][:], in_=psb[pi][:]), reads=[("ps", pi)], writes=[("kvst", st)])
                for h4 in range(4):
                    add("act", lambda h, st=st, gt=gt, h4=h4: h.copy(out=Vaug[:, gt, h4, 0:128], in_=kvst[st][:, h4 * 128:(h4 + 1) * 128]),
                        reads=[("kvst", st)], writes=[("V", gt, h4)])
                add("sp", lambda h, st=st, gt=gt: h.dma_start(out=vp[gt * 128:(gt + 1) * 128, :], in_=kvst[st][:]),
                    reads=[("kvst", st)], dma_key=("kvst", st))
            if hs == 1:
                for b in range(4):
                    pi = proj_tm(wv, lambda kc, b=b: hTs[:, kc, b * 8:(b + 1) * 8], 8, ["hTs"])
                    st = stc[0] % 2
                    stc[0] += 1
                    add("dve", lambda h, pi=pi, st=st: h.tensor_copy(out=kvst[st][0:8], in_=psb[pi][0:8, :]), reads=[("ps", pi)], writes=[("kvst", st)])
                    for h4 in range(4):
                        add("act", lambda h, st=st, b=b, h4=h4: h.copy(out=vaugn[0:8, b, h4, 0:128], in_=kvst[st][0:8, h4 * 128:(h4 + 1) * 128]),
                            reads=[("kvst", st)], writes=[("Vn", b)])
                    add("sp", lambda h, st=st, b=b: h.dma_start(out=vs[b * 8:(b + 1) * 8, :], in_=kvst[st][0:8]),
                        reads=[("kvst", st)], dma_key=("kvst", st))

            wq = load_w(win_block(1536))
            for cj in range(2):
                gj = hs * 2 + cj
                for hh in range(4):
                    pi = proj_fm(wq, hh, lambda kc, cj=cj: hT[:, kc, cj * 512:(cj + 1) * 512], 512, hkeys[cj * 4:cj * 4 + 4])
                    add("act", lambda h, pi=pi, hh=hh, gj=gj: h.copy(out=QT[:, hh, gj * 512:(gj + 1) * 512], in_=psb[pi][:]),
                        reads=[("ps", pi)], writes=[("QT", gj, hh)])
            if hs == 1:
                for hh in range(4):
                    pi = proj_fm(wq, hh, lambda kc: hTs[:, kc, :], 32, ["hTs"])
                    for c in range(2):
                        add("act", lambda h, pi=pi, hh=hh, c=c: h.activation(
                            out=qbd[c * 64:(c + 1) * 64, :, hh, hh * 16 + c * 8: hh * 16 + c * 8 + 8],
                            in_=psb[pi][c * 64:(c + 1) * 64, 0:32].rearrange("p (b q) -> p b q", b=4),
                            func=AF.Copy, scale=0.125), reads=[("ps", pi)], writes=["qbd"])

            wB = load_w(win_block(0))
            wC = load_w(win_block(512))
            wU = load_w(win_block(1024))
            for cj in range(2):
                gj = hs * 2 + cj
                for cc in range(4):
                    if hs == 0 and cj == 1:
                        p0_tile(1, cc)
                    rk = hkeys[cj * 4:cj * 4 + 4]
                    rhs = lambda kc, cj=cj: hT[:, kc, cj * 512:(cj + 1) * 512]
                    pC = proj_fm(wC, cc, rhs, 512, rk)
                    pU = proj_fm(wU, cc, rhs, 512, rk)
                    pB = proj_fm(wB, cc, rhs, 512, rk)
                    add("act", lambda h, pC=pC: h.copy(out=csb[:], in_=psb[pC][:]), reads=[("ps", pC)], writes=["csb"])
                    if gj > 0:
                        add("dve", lambda h, cc=cc: h.tensor_copy(out=cu[:, cc, 0:2], in_=cu[:, cc, 512:514]),
                            reads=[("cu", cc)], writes=[("cuh", cc)])
                    add("dve", lambda h, cc=cc, pU=pU: h.tensor_tensor(out=cu[:, cc, 2:514], in0=psb[pU][:], in1=csb[:], op=ALU.mult),
                        reads=[("ps", pU), "csb", ("cuh", cc)], writes=[("cu", cc)])
                    add("dve", lambda h, cc=cc: h.tensor_scalar(out=ct1[:], in0=cu[:, cc, 0:512], scalar1=convw[:, cc, 0:1], scalar2=None, op0=ALU.mult),
                        reads=[("cu", cc), ("cuh", cc), ("convw", 0), ("convw", 1), ("convw", 2), ("convw", 3)], writes=["ct1"])
                    add("dve", lambda h, cc=cc: h.scalar_tensor_tensor(out=ct2[:], in0=cu[:, cc, 1:513], scalar=convw[:, cc, 1:2], in1=ct1[:],
                                                                       op0=ALU.mult, op1=ALU.add),
                        reads=[("cu", cc), ("cuh", cc), "ct1", ("convw", 0), ("convw", 1), ("convw", 2), ("convw", 3)], writes=["ct2"])
                    add("dve", lambda h, cc=cc: h.scalar_tensor_tensor(out=ct1[:], in0=cu[:, cc, 2:514], scalar=convw[:, cc, 2:3], in1=ct2[:],
                                                                       op0=ALU.mult, op1=ALU.add),
                        reads=[("cu", cc), "ct2", ("convw", 0), ("convw", 1), ("convw", 2), ("convw", 3)], writes=["ct1"])
                    add("dve", lambda h, cc=cc, pB=pB, gj=gj: h.tensor_tensor(out=mixT[:, cc, gj * 512:(gj + 1) * 512], in0=ct1[:], in1=psb[pB][:], op=ALU.mult),
                        reads=["ct1", ("ps", pB)], writes=[("mixc", cc, gj)])
            if hs == 1:
                for cc in range(4):
                    add("sp", lambda h, cc=cc: h.dma_start(out=cp[:, cc * 128:(cc + 1) * 128].rearrange("r p -> p r"), in_=cu[:, cc, 512:514],
                                                           allow_slow_non_contiguous=True),
                        reads=[("cu", cc)], dma_key="cp")
                for cc in range(4):
                    for b in range(4):
                        add("sp", lambda h, cc=cc, b=b: h.dma_start(out=cus[:, cc, b, 0:2], in_=sconv[b, :, cc * 128:(cc + 1) * 128].rearrange("r p -> p r"),
                                                                    allow_slow_non_contiguous=True), writes=[("cus_h", cc, b)], dma_key=("cus_h", cc))
                for cc in range(4):
                    rhs = lambda kc: hTs[:, kc, :]
                    pC = proj_fm(wC, cc, rhs, 32, ["hTs"])
                    pU = proj_fm(wU, cc, rhs, 32, ["hTs"])
                    pB = proj_fm(wB, cc, rhs, 32, ["hTs"])
                    v3 = lambda ap: ap.rearrange("p (b q) -> p b q", b=4)
                    add("act", lambda h, pC=pC: h.copy(out=csb[:, 0:32], in_=psb[pC][:, 0:32]), reads=[("ps", pC)], writes=["csb"])
                    add("dve", lambda h, cc=cc, pU=pU: h.tensor_tensor(out=cus[:, cc, :, 2:10], in0=v3(psb[pU][:, 0:32]), in1=v3(csb[:, 0:32]), op=ALU.mult),
                        reads=[("ps", pU), "csb"], writes=[("cus", cc)])
                    add("dve", lambda h, cc=cc: h.tensor_scalar(out=v3(ct1[:, 0:32]), in0=cus[:, cc, :, 0:8], scalar1=convw[:, cc, 0:1], scalar2=None, op0=ALU.mult),
                        reads=[("cus", cc)] + [("cus_h", cc, b_) for b_ in range(4)] + [("convw", 0), ("convw", 1), ("convw", 2), ("convw", 3)], writes=["ct1"])
                    add("dve", lambda h, cc=cc: h.scalar_tensor_tensor(out=v3(ct2[:, 0:32]), in0=cus[:, cc, :, 1:9], scalar=convw[:, cc, 1:2], in1=v3(ct1[:, 0:32]),
                                                                       op0=ALU.mult, op1=ALU.add),
                        reads=[("cus", cc), "ct1"] + [("cus_h", cc, b_) for b_ in range(4)], writes=["ct2"])
                    add("dve", lambda h, cc=cc: h.scalar_tensor_tensor(out=v3(ct1[:, 0:32]), in0=cus[:, cc, :, 2:10], scalar=convw[:, cc, 2:3], in1=v3(ct2[:, 0:32]),
                                                                       op0=ALU.mult, op1=ALU.add),
                        reads=[("cus", cc), "ct2"], writes=["ct1"])
                    add("dve", lambda h, cc=cc, pB=pB: h.tensor_tensor(out=mixTs[:, cc, :], in0=ct1[:, 0:32], in1=psb[pB][:, 0:32], op=ALU.mult),
                        reads=["ct1", ("ps", pB)], writes=[("mixcs", cc)])
                for cc in range(4):
                    for b in range(4):
                        add("sp", lambda h, cc=cc, b=b: h.dma_start(out=cs[b, :, cc * 128:(cc + 1) * 128].rearrange("r p -> p r"), in_=cus[:, cc, b, 8:10],
                                                                    allow_slow_non_contiguous=True),
                            reads=[("cus", cc)], dma_key="cs")

        if stage < 2:
            P.emit(nc, es)
            return nc
        ov_keys = [("hT", t) for t in range(8)] + [("W", s_) for s_ in range(4)] + [("kvst", 0), ("kvst", 1)]
        ov_keys += ([("cu", c_) for c_ in range(4)] + [("cuh", c_) for c_ in range(4)] + ["csb", "ct1", "ct2"]
                    + [("xin", 0), ("xin", 1), ("hb", 0), ("hb", 1), ("ss", 0), ("ss", 1)])
        B2 = Alloc(PH)
        PT = [B2([128, 2, 512], BF16) for _ in range(3)]
        obuf = B2([128, 4, 512], F32)
        obf = [B2([128, 512], BF16) for _ in range(2)]
        tbuf = [B2([128, 128], F32) for _ in range(2)]
        rl = B2([128, 64], F32)
        ssa = B2([128, 16], F32)
        zer = B2([128, 512], BF16)
        NKV = 16
        BK = Alloc(CU_OFF)
        kvp = [BK([128, 1028], BF16) for _ in range(NKV)]
        assert BK.o <= ARENA_B, BK.o
        Psn = B2([128, 8], BF16)
        PTn = B2([128, 64], BF16)
        KTs = [B2([128, 4, 512], BF16) for _ in range(2)]
        Psb = [B2([128, 512], BF16) for _ in range(2)]
        PTs = [B2([128, 4, 64], BF16) for _ in range(2)]
        asb = B2([128, 512], F32)
        osb = B2([128, 512], F32)
        osbb = B2([128, 512], BF16)
        asbb = B2([128, 512], BF16)
        selb = B2([128, 8], BF16)
        assert B2.o <= PH + 48 * KB, B2.o

        first_ov = [True]

        def ovw(keys):
            if first_ov[0]:
                first_ov[0] = False
                return list(keys) + ov_keys
            return list(keys)

        add("pool", lambda h: h.memset(zer[:], 0.0), writes=ovw(["zer"]) + [("PT", i, c_, b_) for i in range(3) for c_ in range(2) for b_ in range(4)]
            + [("obuf", m_, h_) for m_ in range(4) for h_ in range(4)] + [("obf", o_, h_) for o_ in range(2) for h_ in range(4)]
            + [("tbuf", 0), ("tbuf", 1)] + [("rl", h_, m_, i_) for h_ in range(4) for m_ in range(4) for i_ in range(2)]
            + [("ssa", m_, h_) for m_ in range(4) for h_ in range(4)] + [("rls", h_) for h_ in range(4)] + ["rls2"] + [("asb", h_) for h_ in range(4)]
            + [("kvp", i) for i in range(16)] + [("KTs", i, jj_) for i in range(2) for jj_ in range(4)]
            + [("Psb", i) for i in range(2)] + [("PTs", i) for i in range(2)] + ["osb", "osbb", "Psn", "PTn"] + [("accs", i) for i in range(3)])
        for i in range(NKV):
            add("pool", lambda h, i=i: h.memset(kvp[i][:, 1024:1028], 1.0), reads=["zer"], writes=[("kvp1", i)])
        add("pool", lambda h: h.memset(asbb[:], 0.0), reads=["zer"], writes=["asbb"])
        add("dve", lambda h: h.tensor_copy(out=selb[:], in_=ctf[:, C_SEL:C_SEL + 8]), reads=["zer", "ctf"], writes=["selb"])

        ACC = [5, 6, 7]
        SB = [(0, 1), (2, 3)]
        TPB = 4

        def acc_ap(sl, npart=128):
            b, pos = divmod(sl, 3)
            return psb[ACC[b]][0:npart, pos * 168: pos * 168 + 129]

        ptc = [0]
        sbc = [0]
        tcnt = [0]
        accs = B2([128, 3, 504], F32)
        assert B2.o <= PH + 48 * KB, B2.o
        units = [(j, hh, kt) for j in range(0 if 'p2' not in skip else 4, 4) for hh in range(4) for kt in range(4 * j + 4)]
        uinfo = {}

        def stS(u):
            j, hh, kt = units[u]
            m0 = max(0, kt - 4 * j)
            qs = m0 * 128
            sp_ = SB[u % 2]
            pt = u % 3
            uinfo[u] = (pt, m0)
            for c in range(2):
                add("pe", lambda h, c=c, sp_=sp_, kt=kt, hh=hh, j=j, qs=qs: h.matmul(
                    psb[sp_[c]][:, qs:512], lhsT=KT[c * 64:(c + 1) * 64, hh, kt * 128:(kt + 1) * 128],
                    rhs=QT[c * 64:(c + 1) * 64, hh, j * 512 + qs:(j + 1) * 512], start=True, stop=True),
                    reads=[("KT", kt // 4, hh), ("QT", j, hh)], writes=[("ps", sp_[c])])
            step = 128 if hh == 0 else 512
            q0 = qs
            while q0 < 512:
                q1 = min(512, q0 + step)
                d_ = ((4 * j + q0 // 128) - kt) if hh == 0 else (4 * j - kt)
                s2 = psall[:, sp_[0] * 512:sp_[0] * 512 + 1024].rearrange("p (c n) -> p c n", c=2)
                add("act", lambda h, s2=s2, pt=pt, q0=q0, q1=q1, d_=d_, hh=hh: h.activation(
                    out=PT[pt][:, :, q0:q1], in_=s2[:, :, q0:q1], func=AF.Exp,
                    bias=ctf[:, C_TABP2 + hh * 19 + d_ + 3: C_TABP2 + hh * 19 + d_ + 4], scale=0.125),
                    reads=[("ps", sp_[0]), ("ps", sp_[1]), "ctf"],
                    writes=[("PT", pt, c_, b_) for c_ in range(2) for b_ in range(q0 // 128, (q1 + 127) // 128)])
                q0 = q1
            if kt >= 4 * j:
                for c in range(2):
                    add("dve", lambda h, pt=pt, qs=qs, c=c: h.tensor_tensor(
                        out=PT[pt][:, c, qs:qs + 128], in0=PT[pt][:, c, qs:qs + 128], in1=maskb[:], op=ALU.mult),
                        reads=["maskb"], writes=[("PT", pt, c, qs // 128)])

        def accs_ap(sl):
            bnk, pos = divmod(sl, 3)
            return accs[:, bnk, pos * 168: pos * 168 + 129]

        def stPV(u):
            j, hh, kt = units[u]
            pt, m0 = uinfo[u]
            if kt == 0:
                for b in ACC:
                    add("pe", lambda h, b=b: h.matmul(psb[b][:], lhsT=zer[:, 0:128], rhs=zer[:], start=True, stop=True),
                        reads=["zer"], writes=[("ps", b)])
            for m in range(m0, 4):
                for c in range(2):
                    sl = 2 * m + c
                    add("pe", lambda h, sl=sl, pt=pt, c=c, m=m, kt=kt, hh=hh: h.matmul(
                        acc_ap(sl), lhsT=PT[pt][:, c, m * 128:(m + 1) * 128], rhs=Vaug[:, kt, hh, 0:129],
                        start=False, stop=(kt == 4 * j + m), skip_group_check=True),
                        reads=[("PT", pt, c, m), ("V", kt, hh), ("V1",)], writes=[("ps", ACC[sl // 3])])
            if kt != 4 * j + 3:
                return
            for bi, b in enumerate(ACC):
                add("dve", lambda h, bi=bi, b=b: h.tensor_copy(out=accs[:, bi, :], in_=psb[b][:, 0:504]), reads=[("ps", b)], writes=[("accs", bi)])
            for m in range(4):
                a0 = accs_ap(2 * m)
                a1 = accs_ap(2 * m + 1)
                k0 = ("accs", (2 * m) // 3)
                k1 = ("accs", (2 * m + 1) // 3)
                r0 = rl[:, (hh * 4 + m) * 2:(hh * 4 + m) * 2 + 1]
                r1 = rl[:, (hh * 4 + m) * 2 + 1:(hh * 4 + m) * 2 + 2]
                tb = tcnt[0] % 2
                tcnt[0] += 1
                add("dve", lambda h, a0=a0, r0=r0: h.reciprocal(out=r0, in_=a0[:, 128:129]), reads=[k0], writes=[("rl", hh, m, 0)])
                add("dve", lambda h, a1=a1, r1=r1: h.reciprocal(out=r1, in_=a1[:, 128:129]), reads=[k1], writes=[("rl", hh, m, 1)])
                add("dve", lambda h, r1=r1: h.tensor_tensor(out=r1, in0=r1, in1=NEGLAM, op=ALU.mult), reads=["neglam"], writes=[("rl", hh, m, 1)])
                add("act", lambda h, a0=a0, r0=r0, tb=tb: h.activation(out=tbuf[tb][:], in_=a0[:, 0:128], func=AF.Copy, scale=r0),
                    reads=[k0, ("rl", hh, m, 0)], writes=[("tbuf", tb)])
                add("dve", lambda h, a1=a1, r1=r1, tb=tb, m=m, hh=hh: h.scalar_tensor_tensor(
                    out=obuf[:, m, hh * 128:(hh + 1) * 128], in0=a1[:, 0:128], scalar=r1, in1=tbuf[tb][:], op0=ALU.mult, op1=ALU.add),
                    reads=[k1, ("rl", hh, m, 1), ("tbuf", tb)], writes=[("obuf", m, hh)])
                add("act", lambda h, tb=tb, m=m, hh=hh: h.activation(out=tbuf[tb][:], in_=obuf[:, m, hh * 128:(hh + 1) * 128], func=AF.Square,
                                                                     accum_out=ssa[:, m * 4 + hh: m * 4 + hh + 1]),
                    reads=[("obuf", m, hh)], writes=[("tbuf", tb), ("ssa", m, hh)])
            if hh != 3:
                return
            ssk = [("ssa", m_, h_) for m_ in range(4) for h_ in range(4)]
            add("dve", lambda h: h.tensor_scalar(out=ssa[:, 0:16], in0=ssa[:, 0:16], scalar1=1.0 / 128, scalar2=EPS, op0=ALU.mult, op1=ALU.add),
                reads=ssk, writes=ssk)
            add("act", lambda h: h.activation(out=ssa[:, 0:16], in_=ssa[:, 0:16], func=AF.Ln), reads=ssk, writes=ssk)
            add("act", lambda h: h.activation(out=ssa[:, 0:16], in_=ssa[:, 0:16], func=AF.Exp, scale=-0.5), reads=ssk, writes=ssk)
            for m in range(4):
                gt = 4 * j + m
                ob = gt % 2
                for h2 in range(4):
                    add("dve", lambda h, m=m, h2=h2, ob=ob: h.scalar_tensor_tensor(
                        out=obf[ob][:, h2 * 128:(h2 + 1) * 128], in0=obuf[:, m, h2 * 128:(h2 + 1) * 128],
                        scalar=ssa[:, m * 4 + h2: m * 4 + h2 + 1], in1=sgbc[:], op0=ALU.mult, op1=ALU.mult),
                        reads=[("obuf", m, h2), ("ssa", m, h2), "sgbc"], writes=[("obf", ob, h2)])
                tp = psb[TPB][:].bitcast(BF16)[:, 0:512].rearrange("p (k n) -> p k n", k=4)
                for h2 in range(4):
                    add("pe", lambda h, h2=h2, ob=ob: h.transpose(out=tp[:, h2, :], in_=obf[ob][:, h2 * 128:(h2 + 1) * 128], identity=identb[:]),
                        reads=[("obf", ob, h2), "identb"], writes=[("ps", TPB)])
                add("act", lambda h, gt=gt: h.copy(out=mixT[:, 4:8, gt * 128:(gt + 1) * 128], in_=tp), reads=[("ps", TPB)], writes=[("mixa", gt)])

        if units:
            stS(0)
            for u in range(len(units)):
                if u + 1 < len(units):
                    stS(u + 1)
                stPV(u)

        SACC = [5, 6]
        SS = [0, 1]
        TK = [2, 3]
        TPP = 4
        OPS = 7

        def sacc_ap(hh):
            return psb[SACC[hh // 2]][0:16, (hh % 2) * 168:(hh % 2) * 168 + 129]

        NG = 64 if 'p2s' not in skip else 0
        vslots = {}

        def stG(i):
            b_, g = divmod(i, 16)
            vsl = []
            for jj in range(4):
                pg = b_ * 64 + 4 * g + jj
                ksl = (4 * i + jj) % NKV
                vsl.append(ksl)
                add("pool", lambda h, ksl=ksl, pg=pg: h.indirect_dma_start(
                    out=kvp[ksl][:, 0:1024], out_offset=None, in_=ckv, in_offset=bass.IndirectOffsetOnAxis(ap=idx[:, pg:pg + 1], axis=0)),
                    reads=["idx"], writes=[("kvp", ksl)], dma_key=("kvp", ksl))
            vslots[i] = vsl

        def stA(i):
            gs = i % 2
            for jj in range(4):
                ksl = vslots[i][jj]
                tki = TK[jj % 2]
                tpk = psb[tki][:].bitcast(BF16)[:, 0:512].rearrange("p (k n) -> p k n", k=4)
                for hh in range(4):
                    add("pe", lambda h, hh=hh, ksl=ksl, tpk=tpk: h.transpose(out=tpk[:, hh, :], in_=kvp[ksl][:, hh * 128:(hh + 1) * 128], identity=identb[:]),
                        reads=[("kvp", ksl), "identb"], writes=[("ps", tki)])
                if jj % 2 == 0:
                    add("act", lambda h, gs=gs, jj=jj, tpk=tpk: h.copy(out=KTs[gs][:, :, jj * 128:(jj + 1) * 128], in_=tpk),
                        reads=[("ps", tki)], writes=[("KTs", gs, jj)])
                else:
                    add("dve", lambda h, gs=gs, jj=jj, tpk=tpk: h.tensor_copy(out=KTs[gs][:, :, jj * 128:(jj + 1) * 128], in_=tpk),
                        reads=[("ps", tki)], writes=[("KTs", gs, jj)])

        def stB(i):
            b_, g = divmod(i, 16)
            gs = i % 2
            si = SS[i % 2]
            for hh in range(4):
                add("pe", lambda h, hh=hh, si=si, gs=gs, b_=b_: h.matmul(psb[si][0:64, :], lhsT=qbd[:, b_, hh, :], rhs=KTs[gs][:, hh, :],
                                                                          start=(hh == 0), stop=False),
                    reads=["qbd"] + [("KTs", gs, jj_) for jj_ in range(4)], writes=[("ps", si)])
            add("pe", lambda h, si=si, g=g: h.matmul(psb[si][0:64, :], lhsT=auxb[:, 512 + g * 64: 512 + (g + 1) * 64], rhs=auxb[:, 0:512],
                                                     start=False, stop=True),
                reads=["auxb"], writes=[("ps", si)])
            add("act", lambda h, si=si, gs=gs: h.activation(out=Psb[gs][0:64, :], in_=psb[si][0:64, :], func=AF.Exp),
                reads=[("ps", si)], writes=[("Psb", gs)])

        def stC(i):
            gs = i % 2
            tpp = psb[TPP][:].bitcast(BF16)[:, 0:256].rearrange("p (k n) -> p k n", k=4)
            for jj in range(4):
                add("pe", lambda h, jj=jj, gs=gs: h.transpose(out=tpp[:, jj, :], in_=Psb[gs][0:64, jj * 128:(jj + 1) * 128], identity=identb[0:64, 0:64]),
                    reads=[("Psb", gs), "identb"], writes=[("ps", TPP)])
            add("dve", lambda h, gs=gs: h.tensor_copy(out=PTs[gs][:], in_=tpp), reads=[("ps", TPP)], writes=[("PTs", gs)])

        def stD(i):
            gs = i % 2
            vsl = vslots[i]
            for jj in range(4):
                for hh in range(4):
                    v_ = vsl[jj]
                    add("pe", lambda h, jj=jj, hh=hh, gs=gs, v_=v_: h.matmul(
                        sacc_ap(hh)[:, 0:128], lhsT=PTs[gs][:, jj, hh * 16:(hh + 1) * 16], rhs=kvp[v_][:, 512 + hh * 128:512 + (hh + 1) * 128], start=False, stop=False,
                        skip_group_check=True),
                        reads=[("PTs", gs), ("kvp", v_)], writes=[("ps", SACC[hh // 2])])
                    add("pe", lambda h, jj=jj, hh=hh, gs=gs, v_=v_: h.matmul(
                        sacc_ap(hh)[:, 128:129], lhsT=PTs[gs][:, jj, hh * 16:(hh + 1) * 16], rhs=kvp[v_][:, 1024:1025], start=False, stop=False,
                        skip_group_check=True),
                        reads=[("PTs", gs), ("kvp", v_), ("kvp1", v_)], writes=[("ps", SACC[hh // 2])])

        def batch_init(b_):
            for bk in SACC:
                add("pe", lambda h, bk=bk: h.matmul(psb[bk][:], lhsT=zer[:, 0:128], rhs=zer[:], start=True, stop=True),
                    reads=["zer"], writes=[("ps", bk)])

        def batch_fin(b):
            si = OPS
            for hh in range(4):
                add("pe", lambda h, hh=hh, b=b: h.matmul(psb[si][0:64, 0:8], lhsT=qbd[:, b, hh, :], rhs=kTn[:, hh, b * 8:(b + 1) * 8],
                                                          start=(hh == 0), stop=False),
                    reads=["qbd", "kTn"], writes=[("ps", si)])
            add("pe", lambda h: h.matmul(psb[si][0:64, 0:8], lhsT=identb[:, 0:64], rhs=newbb[:, :], start=False, stop=True),
                reads=["identb", "newbb"], writes=[("ps", si)])
            add("act", lambda h: h.activation(out=Psn[0:64, 0:8], in_=psb[si][0:64, 0:8], func=AF.Exp), reads=[("ps", si)], writes=["Psn"])
            tpn = psb[TPP][:].bitcast(BF16)[0:8, 0:64]
            add("pe", lambda h: h.transpose(out=tpn, in_=Psn[0:64, 0:8], identity=identb[0:64, 0:64]),
                reads=["Psn", "identb"], writes=[("ps", TPP)])
            add("dve", lambda h: h.tensor_copy(out=PTn[0:8, :], in_=tpn), reads=[("ps", TPP)], writes=["PTn"])
            for hh in range(4):
                add("pe", lambda h, hh=hh, b=b: h.matmul(sacc_ap(hh), lhsT=PTn[0:8, hh * 16:(hh + 1) * 16], rhs=vaugn[0:8, b, hh, 0:129],
                                                          start=False, stop=True, skip_group_check=True),
                    reads=["PTn", ("Vn", b), ("Vn1",)], writes=[("ps", SACC[hh // 2])])
            for hh in range(4):
                a_ = sacc_ap(hh)
                rr = rl[0:16, 40 + hh: 41 + hh]
                add("dve", lambda h, a_=a_, rr=rr: h.reciprocal(out=rr, in_=a_[:, 128:129]), reads=[("ps", SACC[hh // 2])], writes=[("rls", hh)])
                add("dve", lambda h, rr=rr: h.tensor_tensor(out=rr, in0=rr, in1=small[0:16, 1:2], op=ALU.mult), reads=["sgn"], writes=[("rls", hh)])
                add("dve", lambda h, a_=a_, rr=rr, hh=hh: h.tensor_scalar(out=asb[0:16, hh * 128:(hh + 1) * 128], in0=a_[:, 0:128], scalar1=rr, scalar2=None, op0=ALU.mult),
                    reads=[("ps", SACC[hh // 2]), ("rls", hh)], writes=[("asb", hh)])
            add("dve", lambda h: h.tensor_copy(out=asbb[0:16, :], in_=asb[0:16, :]), reads=[("asb", h_) for h_ in range(4)], writes=["asbb"])
            add("pe", lambda h: h.matmul(psb[OPS][0:8, :], lhsT=selb[0:32, :], rhs=asbb[0:32, :], start=True, stop=True),
                reads=["selb", "asbb"], writes=[("ps", OPS)])
            add("dve", lambda h: h.tensor_copy(out=osb[0:8, :], in_=psb[OPS][0:8, :]), reads=[("ps", OPS)], writes=["osb"])
            for hh in range(4):
                add("act", lambda h, hh=hh: h.activation(out=asb[0:8, hh * 128:(hh + 1) * 128], in_=osb[0:8, hh * 128:(hh + 1) * 128], func=AF.Square,
                                                         accum_out=rl[0:8, 48 + hh:49 + hh]),
                    reads=["osb"], writes=[("asb", hh), "rls2"])
            rstd_ops(rl[0:8, 48:52], 128, "rls2")
            for hh in range(4):
                add("dve", lambda h, hh=hh: h.scalar_tensor_tensor(out=osbb[0:8, hh * 128:(hh + 1) * 128], in0=osb[0:8, hh * 128:(hh + 1) * 128],
                                                                   scalar=rl[0:8, 48 + hh:49 + hh], in1=sgbc[0:8, :], op0=ALU.mult, op1=ALU.mult),
                    reads=["osb", "rls2", "sgbc"], writes=["osbb"])
            tpo = psb[TPP][:].bitcast(BF16)[:, 0:32].rearrange("p (k n) -> p k n", k=4)
            for hh in range(4):
                add("pe", lambda h, hh=hh: h.transpose(out=tpo[:, hh, :], in_=osbb[0:8, hh * 128:(hh + 1) * 128], identity=identb[0:8, 0:8]),
                    reads=["osbb", "identb"], writes=[("ps", TPP)])
            add("act", lambda h, b=b: h.copy(out=mixTs[:, 4:8, b * 8:(b + 1) * 8], in_=tpo), reads=[("ps", TPP)], writes=[("mixas", b)])

        def do_D(i):
            if i % 16 == 0:
                batch_init(i // 16)
            stD(i)
            if i % 16 == 15:
                batch_fin(i // 16)

        if NG:
            stG(0)
            stG(1)
            stA(0)
            for i in range(NG):
                if i + 2 < NG:
                    stG(i + 2)
                if i + 1 < NG:
                    stA(i + 1)
                stB(i)
                if i >= 1:
                    do_D(i - 1)
                stC(i)
            do_D(NG - 1)

        if stage < 4:
            P.emit(nc, es)
            return nc
        p2_keys = (["zer", ("tbuf", 0), ("tbuf", 1), "osb", "osbb", "asbb", "selb", "rls2"] + [("accs", i) for i in range(3)]
                   + [("obuf", m_, h_) for m_ in range(4) for h_ in range(4)] + [("obf", o_, h_) for o_ in range(2) for h_ in range(4)]
                   + [("rl", h_, m_, i_) for h_ in range(4) for m_ in range(4) for i_ in range(2)]
                   + [("ssa", m_, h_) for m_ in range(4) for h_ in range(4)] + [("rls", h_) for h_ in range(4)] + [("asb", h_) for h_ in range(4)]
                   + [("PT", i, c_, b_) for i in range(3) for c_ in range(2) for b_ in range(4)] + [("kvp", i) for i in range(16)] + [("kvp1", i) for i in range(16)] + ["Psn", "PTn"]
                   + [("KTs", i, jj_) for i in range(2) for jj_ in range(4)] + [("Psb", i) for i in range(2)] + [("PTs", i) for i in range(2)]
                   + [("KT", j_, h_) for j_ in range(4) for h_ in range(4)] + [("QT", j_, h_) for j_ in range(4) for h_ in range(4)]
                   + [("V", t, h_) for t in range(16) for h_ in range(4)] + [("V1",)] + [("cu", c_) for c_ in range(4)] + [("cuh", c_) for c_ in range(4)]
                   + ["csb", "ct1", "ct2", ("kvst", 0), ("kvst", 1), ("xin", 0), ("xin", 1), ("hb", 0), ("hb", 1)]
                   + [("ss", 0), ("ss", 1)])
        B3 = Alloc(PH)
        x1 = B3([128, 16, 1024], F32)
        h2T = B3([128, 8, S], BF16)
        Wo = B3([128, 8, 1024], BF16)
        xin3 = [B3([128, 1024], F32) for _ in range(2)]
        hb3 = [B3([128, 1024], BF16) for _ in range(2)]
        ss3 = B3([128, 32], F32)
        A2_END = B3.o
        assert A2_END <= ARENA_B, A2_END

        add("pool", lambda h: h.dma_start(out=Wo[:], in_=w_out.rearrange("(k p) n -> p k n", p=128)),
            writes=["Wo"] + p2_keys + [("x1", t) for t in range(16)] + [("h2T", t) for t in range(16)]
            + [("xin3", 0), ("xin3", 1), ("hb3", 0), ("hb3", 1), ("ss3", 0), ("ss3", 1)], dma_key="Wo")
        add("sp", lambda h: h.dma_start(out=gbc[:], in_=n2g.partition_broadcast(128)), writes=["gbc"], dma_key="c5")

        def outproj_tile(xsrc, npart, lhs_ap, lkeys, x1ap, x1key, slot, dstT, dkeys, use_act):
            add("sp", lambda h: h.dma_start(out=xin3[slot][:npart], in_=xsrc), writes=[("xin3", slot)], dma_key=("xin3", slot))
            pis = []
            for nh in range(2):
                pi = nextps()
                pis.append(pi)
                for fc in range(8):
                    add("pe", lambda h, fc=fc, pi=pi, nh=nh: h.matmul(psb[pi][0:npart, :], lhsT=lhs_ap(fc), rhs=Wo[:, fc, nh * 512:(nh + 1) * 512],
                                                                       start=(fc == 0), stop=(fc == 7)),
                        reads=["Wo"] + lkeys, writes=[("ps", pi)])
            for nh in range(2):
                add("dve", lambda h, nh=nh, pi=pis[nh]: h.tensor_tensor(out=x1ap[:, nh * 512:(nh + 1) * 512], in0=psb[pi][0:npart, :],
                                                                         in1=xin3[slot][:npart, nh * 512:(nh + 1) * 512], op=ALU.add),
                    reads=[("ps", pis[nh]), ("xin3", slot)], writes=[x1key])
            sa = ss3[:npart, slot:slot + 1]
            add("act", lambda h: h.activation(out=hb3[slot][:npart], in_=x1ap, func=AF.Square, accum_out=sa),
                reads=[x1key], writes=[("hb3", slot), ("ss3", slot)])
            rstd_ops(sa, D, ("ss3", slot))
            add("dve", lambda h: h.scalar_tensor_tensor(out=hb3[slot][:npart], in0=x1ap, scalar=sa, in1=gbc[:npart], op0=ALU.mult, op1=ALU.mult),
                reads=[x1key, ("ss3", slot), "gbc"], writes=[("hb3", slot)])

            def part2():
                pi = nextps()
                tp = psb[pi][:].bitcast(BF16)[:, 0:8 * 128].rearrange("p (k n) -> p k n", k=8)
                for kc in range(8):
                    add("pe", lambda h, kc=kc: h.transpose(out=tp[:, kc, 0:npart], in_=hb3[slot][:npart, kc * 128:(kc + 1) * 128],
                                                           identity=identb[:npart, :npart]),
                        reads=[("hb3", slot), "identb"], writes=[("ps", pi)])
                if use_act:
                    add("act", lambda h: h.copy(out=dstT, in_=tp[:, :, 0:npart]), reads=[("ps", pi)], writes=dkeys)
                else:
                    add("dve", lambda h: h.tensor_copy(out=dstT, in_=tp[:, :, 0:npart]), reads=[("ps", pi)], writes=dkeys)
            return part2

        pend = None
        for t in range(16):
            lk = [("mixc", cc, t // 4) for cc in range(4)] + [("mixa", t)]
            p2_ = outproj_tile(xp[t * 128:(t + 1) * 128, :], 128, lambda fc, t=t: mixT[:, fc, t * 128:(t + 1) * 128], lk,
                               x1[:, t, :], ("x1", t), t % 2, h2T[:, :, t * 128:(t + 1) * 128], [("h2T", t)], t % 2 == 0)
            if pend is not None:
                pend()
            pend = p2_
        p2_ = outproj_tile(xs, 32, lambda fc: mixTs[:, fc, :], [("mixcs", cc) for cc in range(4)] + [("mixas", b) for b in range(4)],
                           x1s[0:32, :], "x1s", 0, h2Ts[:], ["h2Ts"], True)
        pend()
        p2_()

        if stage < 5:
            P.emit(nc, es)
            return nc
        a2_tail = ["Wo", ("xin3", 0), ("xin3", 1), ("hb3", 0), ("hb3", 1), ("ss3", 0), ("ss3", 1)]
        mix_keys = [("mixc", cc, j_) for cc in range(4) for j_ in range(4)] + [("mixa", t) for t in range(16)]
        BM = Alloc(MIX_OFF)
        Wu = [BM([128, 8, 512], BF16) for _ in range(2)]
        Wd = [BM([128, 4, 1024], BF16) for _ in range(2)]
        assert BM.o <= PH
        B4 = Alloc(PH + 96 * KB)
        aT = [B4([128, 4, 512], BF16) for _ in range(2)]
        aTs = B4([128, 4, 32], BF16)
        sq = [B4([128, 512], F32) for _ in range(2)]
        yst = [B4([128, 1024], F32) for _ in range(2)]
        ss4 = B4([128, 32], F32)
        assert B4.o <= ARENA_B, B4.o

        add("sp", lambda h: h.dma_start(out=gbc[:], in_=fing.partition_broadcast(128)), writes=["gbc"], dma_key="c5")

        def final_tile(x1ap, x1key, npart, slot, dst):
            sa = ss4[:npart, slot:slot + 1]
            add("act", lambda h: h.activation(out=yst[slot][:npart], in_=x1ap, func=AF.Square, accum_out=sa),
                reads=[x1key], writes=[("yst", slot), ("ss4", slot)])
            rstd_ops(sa, D, ("ss4", slot))
            add("dve", lambda h: h.scalar_tensor_tensor(out=yst[slot][:npart], in0=x1ap, scalar=sa, in1=gbc[:npart], op0=ALU.mult, op1=ALU.mult),
                reads=[x1key, ("ss4", slot), "gbc"], writes=[("yst", slot)])
            add("sp", lambda h: h.dma_start(out=dst, in_=yst[slot][:npart]), reads=[("yst", slot)], dma_key=("yst", slot))


        sqc = [0]
        for fb in range(8):
            ws = fb % 2
            extra = (mix_keys + a2_tail + [("Wu", 1), ("Wd", 0), ("Wd", 1)] + [("aT", a_, f_) for a_ in range(2) for f_ in range(4)] + ["aTs", ("sq", 0), ("sq", 1),
                                           ("yst", 0), ("yst", 1), ("ss4", 0), ("ss4", 1)]) if fb == 0 else []
            add("pool", lambda h, ws=ws, fb=fb: h.dma_start(out=Wu[ws][:], in_=w_up[:, fb * 512:(fb + 1) * 512].rearrange("(k p) n -> p k n", p=128)),
                writes=[("Wu", ws)] + extra, dma_key=("Wu", ws))
            add("pool", lambda h, ws=ws, fb=fb: h.dma_start(out=Wd[ws][:], in_=w_down[fb * 512:(fb + 1) * 512, :].rearrange("(k p) n -> p k n", p=128)),
                writes=[("Wd", ws)], dma_key=("Wd", ws))
            for j in range(4):
                as_ = (fb * 4 + j) % 2
                for fc in range(4):
                    pi = nextps()
                    for kc in range(8):
                        add("pe", lambda h, kc=kc, pi=pi, fc=fc, ws=ws, j=j: h.matmul(psb[pi][:], lhsT=Wu[ws][:, kc, fc * 128:(fc + 1) * 128],
                                                                                      rhs=h2T[:, kc, j * 512:(j + 1) * 512], start=(kc == 0), stop=(kc == 7)),
                            reads=[("Wu", ws)] + [("h2T", 4 * j + i) for i in range(4)], writes=[("ps", pi)])
                    sqs = sqc[0] % 2
                    sqc[0] += 1
                    add("act", lambda h, pi=pi, sqs=sqs: h.activation(out=sq[sqs][:], in_=psb[pi][:], func=AF.Square),
                        reads=[("ps", pi)], writes=[("sq", sqs)])
                    add("dve", lambda h, pi=pi, sqs=sqs, as_=as_, fc=fc: h.scalar_tensor_tensor(
                        out=aT[as_][:, fc, :], in0=psb[pi][:], scalar=0.0, in1=sq[sqs][:], op0=ALU.is_gt, op1=ALU.mult),
                        reads=[("ps", pi), ("sq", sqs)], writes=[("aT", as_, fc)])
                for tt in range(4):
                    t = 4 * j + tt
                    pis = []
                    for nh in range(2):
                        pi = nextps()
                        pis.append(pi)
                        for fc in range(4):
                            add("pe", lambda h, fc=fc, pi=pi, nh=nh, as_=as_, tt=tt, ws=ws: h.matmul(
                                psb[pi][:], lhsT=aT[as_][:, fc, tt * 128:(tt + 1) * 128], rhs=Wd[ws][:, fc, nh * 512:(nh + 1) * 512],
                                start=(fc == 0), stop=(fc == 3)),
                                reads=[("aT", as_, fc), ("Wd", ws)], writes=[("ps", pi)])
                    for nh in range(2):
                        add("dve", lambda h, nh=nh, pi=pis[nh], t=t: h.tensor_tensor(out=x1[:, t, nh * 512:(nh + 1) * 512], in0=psb[pi][:],
                                                                                      in1=x1[:, t, nh * 512:(nh + 1) * 512], op=ALU.add),
                            reads=[("ps", pis[nh])], writes=[("x1", t)])
                    if fb == 7:
                        final_tile(x1[:, t, :], ("x1", t), 128, t % 2, yp[t * 128:(t + 1) * 128, :])
            for fc in range(4):
                pi = nextps()
                for kc in range(8):
                    add("pe", lambda h, kc=kc, pi=pi, fc=fc, ws=ws: h.matmul(psb[pi][:, 0:32], lhsT=Wu[ws][:, kc, fc * 128:(fc + 1) * 128],
                                                                             rhs=h2Ts[:, kc, :], start=(kc == 0), stop=(kc == 7)),
                        reads=[("Wu", ws), "h2Ts"], writes=[("ps", pi)])
                sqs = sqc[0] % 2
                sqc[0] += 1
                add("act", lambda h, pi=pi, sqs=sqs: h.activation(out=sq[sqs][:, 0:32], in_=psb[pi][:, 0:32], func=AF.Square),
                    reads=[("ps", pi)], writes=[("sq", sqs)])
                add("dve", lambda h, pi=pi, sqs=sqs, fc=fc: h.scalar_tensor_tensor(
                    out=aTs[:, fc, :], in0=psb[pi][:, 0:32], scalar=0.0, in1=sq[sqs][:, 0:32], op0=ALU.is_gt, op1=ALU.mult),
                    reads=[("ps", pi), ("sq", sqs)], writes=["aTs"])
            pis = []
            for nh in range(2):
                pi = nextps()
                pis.append(pi)
                for fc in range(4):
                    add("pe", lambda h, fc=fc, pi=pi, nh=nh, ws=ws: h.matmul(psb[pi][0:32, :], lhsT=aTs[:, fc, :], rhs=Wd[ws][:, fc, nh * 512:(nh + 1) * 512],
                                                                             start=(fc == 0), stop=(fc == 3)),
                        reads=["aTs", ("Wd", ws)], writes=[("ps", pi)])
            for nh in range(2):
                add("dve", lambda h, nh=nh, pi=pis[nh]: h.tensor_tensor(out=x1s[0:32, nh * 512:(nh + 1) * 512], in0=psb[pi][0:32, :],
                                                                         in1=x1s[0:32, nh * 512:(nh + 1) * 512], op=ALU.add),
                    reads=[("ps", pis[nh])], writes=["x1s"])

        final_tile(x1s[0:32, :], "x1s", 32, 0, ys)

        P.emit(nc, es)
    return nc


_NC = None


def kernel(x_prompt, x_sample, cache_k, cache_v, state_conv, page_table, norm1_g, w_in, conv_w,
           lambda_q1, lambda_k1, lambda_q2, lambda_k2, subln_g, w_out, norm2_g, w_up, w_down, final_g):
    global _NC
    if _NC is None:
        _NC = build()
    nc = _NC
    f = lambda a: np.ascontiguousarray(np.asarray(a), dtype=np.float32)
    ckv = np.concatenate([f(cache_k).reshape(NPOOL * 128, 512), f(cache_v).reshape(NPOOL * 128, 512)], axis=1)
    lamv = np.concatenate([f(lambda_q1)[0], f(lambda_k1)[0], f(lambda_q2)[0], f(lambda_k2)[0]])[None, :]
    ctab = make_ctab()
    shared = dict(ckv=ckv, n1g=f(norm1_g), w_in=f(w_in)[0], conv_w=f(conv_w)[0], lamv=lamv, subg=f(subln_g),
                  w_out=f(w_out)[0], n2g=f(norm2_g), w_up=f(w_up)[0], w_down=f(w_down)[0], fing=f(final_g)[None, :], ctab=ctab)
    xp = f(x_prompt)
    xs = f(x_sample)
    sc = f(state_conv)[0]
    pt = np.ascontiguousarray(np.asarray(page_table), dtype=np.int32)
    in_maps = []
    for c in range(NCORES):
        m = dict(shared)
        m["xp"] = xp[c]
        m["xs"] = xs[4 * c:4 * c + 4].reshape(32, D)
        m["sconv"] = sc[4 * c:4 * c + 4]
        m["ptab"] = pt[4 * c:4 * c + 4].reshape(1, 256)
        in_maps.append(m)
    res = run_bass_kernel_spmd(nc, in_maps, core_ids=list(range(NCORES)))
    r = res.results
    y_prompt = np.stack([r[c]["yp"] for c in range(NCORES)])
    y_sample = np.concatenate([r[c]["ys"].reshape(4, 8, D) for c in range(NCORES)])
    k_prompt = np.stack([r[c]["kp"].reshape(S, 4, 2, 64) for c in range(NCORES)])[None]
    v_prompt = np.stack([r[c]["vp"].reshape(S, 4, 128) for c in range(NCORES)])[None]
    conv_prompt = np.stack([r[c]["cp"] for c in range(NCORES)])[None]
    k_sample = np.concatenate([r[c]["ks"].reshape(4, 8, 4, 2, 64) for c in range(NCORES)])[None]
    v_sample = np.concatenate([r[c]["vs"].reshape(4, 8, 4, 128) for c in range(NCORES)])[None]
    conv_sample = np.concatenate([r[c]["cs"] for c in range(NCORES)])[None]
    return (y_prompt.astype(np.float32), y_sample.astype(np.float32), k_prompt.astype(np.float32), v_prompt.astype(np.float32),
            conv_prompt.astype(np.float32), k_sample.astype(np.float32), v_sample.astype(np.float32), conv_sample.astype(np.float32))
```
